# Optimizing a Trainium2 kernel written in Bass

```python
import jax, jax.numpy as jnp
from jax import lax
import numpy as np

D_MODEL = 2048
BATCH = 8
SEQ = 2048
DEPTH = 1

MEM_LEN = 256
CONV_WIDTH = D_MODEL // 4
CONV_GROUPS = 8
CONV_K = 3
DIFF_WIDTH = D_MODEL // 2
DIFF_VDIM = 128
DIFF_HALF = 64
DIFF_HEADS = DIFF_WIDTH // DIFF_VDIM
MEM_WIDTH = D_MODEL // 4
MEM_HEADS = 4
MEM_HEAD_DIM = MEM_WIDTH // MEM_HEADS
MIX_WIDTH = CONV_WIDTH + DIFF_WIDTH + MEM_WIDTH
IN_SPLITS = [CONV_WIDTH, CONV_WIDTH, CONV_WIDTH,
             DIFF_WIDTH, DIFF_WIDTH, DIFF_WIDTH, MEM_WIDTH]
IN_WIDTH = sum(IN_SPLITS)
ROT_DIM = DIFF_HALF // 4
ROPE_THETA = 500000.0
FFN_HIDDEN = -(-(8 * D_MODEL) // (3 * 256)) * 256
Q_BLOCK = 128
EPS = 1e-6

kernel_name = "hybrid_conv_diffattn_memxattn_layer"


def rmsnorm(x, g):
    xf = x.astype(jnp.float32)
    r = lax.rsqrt(jnp.mean(xf * xf, axis=-1, keepdims=True) + EPS)
    return (xf * r * g.astype(jnp.float32)).astype(x.dtype)


def lambda_init(layer_idx):
    return 0.8 - 0.6 * float(np.exp(-0.3 * (layer_idx - 1)))


def rope_tables(positions):
    inv_freq = ROPE_THETA ** (-jnp.arange(0, ROT_DIM, 2, dtype=jnp.float32) / ROT_DIM)
    ang = positions.astype(jnp.float32)[..., None] * inv_freq
    return jnp.cos(ang), jnp.sin(ang)


def apply_partial_rope(t, cos, sin):
    c = cos[:, :, None, None, :]
    s = sin[:, :, None, None, :]
    tf = t.astype(jnp.float32)
    half = ROT_DIM // 2
    r1, r2, rest = tf[..., :half], tf[..., half:ROT_DIM], tf[..., ROT_DIM:]
    out = jnp.concatenate([r1 * c - r2 * s, r2 * c + r1 * s, rest], axis=-1)
    return out.astype(t.dtype)


def short_gated_conv(u, c_gate, b_gate, conv_w):
    z = c_gate * u
    w = conv_w[:, None, :].astype(z.dtype)
    conv = lax.conv_general_dilated(
        z, w, window_strides=(1,), padding=[(CONV_K - 1, 0)],
        dimension_numbers=("NWC", "WIO", "NWC"), feature_group_count=CONV_WIDTH)
    return b_gate * conv


def diff_attention(q, k, v, cos, sin, g_q, g_k, lq1, lk1, lq2, lk2, g_sub, lam_init):
    B, S, _ = q.shape
    q = q.reshape(B, S, DIFF_HEADS, 2, DIFF_HALF)
    k = k.reshape(B, S, DIFF_HEADS, 2, DIFF_HALF)
    vf = v.reshape(B, S, DIFF_HEADS, DIFF_VDIM).astype(jnp.float32)
    q = apply_partial_rope(rmsnorm(q, g_q), cos, sin)
    k = apply_partial_rope(rmsnorm(k, g_k), cos, sin)
    lam = (jnp.exp(jnp.sum(lq1.astype(jnp.float32) * lk1.astype(jnp.float32)))
           - jnp.exp(jnp.sum(lq2.astype(jnp.float32) * lk2.astype(jnp.float32)))
           + lam_init)
    scale = DIFF_HALF ** -0.5
    key_pos = jnp.arange(S)

    def one_block(i):
        start = i * Q_BLOCK
        qb = lax.dynamic_slice_in_dim(q, start, Q_BLOCK, axis=1)
        s = jnp.einsum('bqhmd,bkhmd->bhmqk', qb, k,
                       preferred_element_type=jnp.float32) * scale
        qpos = start + jnp.arange(Q_BLOCK)
        mask = qpos[:, None] >= key_pos[None, :]
        p = jax.nn.softmax(jnp.where(mask, s, -jnp.inf), axis=-1)
        a = p[:, :, 0] - lam * p[:, :, 1]
        return jnp.einsum('bhqk,bkhd->bqhd', a, vf)

    o = lax.map(one_block, jnp.arange(S // Q_BLOCK))
    o = jnp.transpose(o, (1, 0, 2, 3, 4)).reshape(B, S, DIFF_HEADS, DIFF_VDIM)
    o = rmsnorm(o, g_sub) * (1.0 - lam_init)
    return o.reshape(B, S, DIFF_WIDTH).astype(v.dtype)


def memory_cross_attention(q_m, kv_m, g_q, g_k):
    B, S, _ = q_m.shape
    q = rmsnorm(q_m.reshape(B, S, MEM_HEADS, MEM_HEAD_DIM), g_q)
    k, v = jnp.split(kv_m, 2, axis=-1)
    k = rmsnorm(k.reshape(B, MEM_LEN, MEM_HEADS, MEM_HEAD_DIM), g_k)
    v = v.reshape(B, MEM_LEN, MEM_HEADS, MEM_HEAD_DIM)
    s = jnp.einsum('bqhd,bmhd->bhqm', q, k,
                   preferred_element_type=jnp.float32) * (MEM_HEAD_DIM ** -0.5)
    p = jax.nn.softmax(s, axis=-1)
    o = jnp.einsum('bhqm,bmhd->bqhd', p, v.astype(jnp.float32))
    return o.reshape(B, S, MEM_WIDTH).astype(q_m.dtype)


def setup_inputs(seed: int = 0) -> dict:
    key = jax.random.key(seed)
    ks = jax.random.split(key, 26)
    f32 = jnp.float32
    L, D = DEPTH, D_MODEL

    def nrm(k, shape, scale):
        return jax.random.normal(k, shape, f32) * scale

    def gain(k, shape):
        return 1.0 + 0.05 * jax.random.normal(k, shape, f32)

    x = jax.random.normal(ks[0], (BATCH, SEQ, D), f32)
    mem = jax.random.normal(ks[1], (BATCH, MEM_LEN, D), f32)
    offsets = jax.random.randint(ks[2], (BATCH, 1), 0, 4096, dtype=jnp.int32)
    positions = offsets + jnp.arange(SEQ, dtype=jnp.int32)[None, :]
    return {
        "x": x,
        "mem": mem,
        "positions": positions,
        "g_mix": gain(ks[3], (L, D)),
        "g_mem": gain(ks[4], (L, D)),
        "w_in": nrm(ks[5], (L, D, IN_WIDTH), D ** -0.5),
        "conv_w": nrm(ks[6], (L, CONV_K, CONV_WIDTH), CONV_K ** -0.5),
        "g_conv_out": gain(ks[7], (L, CONV_WIDTH)),
        "g_dq": gain(ks[8], (L, DIFF_HALF)),
        "g_dk": gain(ks[9], (L, DIFF_HALF)),
        "lam_q1": nrm(ks[10], (L, DIFF_HALF), 0.1),
        "lam_k1": nrm(ks[11], (L, DIFF_HALF), 0.1),
        "lam_q2": nrm(ks[12], (L, DIFF_HALF), 0.1),
        "lam_k2": nrm(ks[13], (L, DIFF_HALF), 0.1),
        "g_sub": gain(ks[14], (L, DIFF_VDIM)),
        "w_mem_kv": nrm(ks[15], (L, D, 2 * MEM_WIDTH), D ** -0.5),
        "g_mq": gain(ks[16], (L, MEM_HEAD_DIM)),
        "g_mk": gain(ks[17], (L, MEM_HEAD_DIM)),
        "g_mem_out": gain(ks[18], (L, MEM_WIDTH)),
        "w_o": nrm(ks[19], (L, MIX_WIDTH, D), MIX_WIDTH ** -0.5),
        "g_ffn": gain(ks[20], (L, D)),
        "w_gate": nrm(ks[21], (L, D, FFN_HIDDEN), D ** -0.5),
        "w_up": nrm(ks[22], (L, D, FFN_HIDDEN), D ** -0.5),
        "w_down": nrm(ks[23], (L, FFN_HIDDEN, D), FFN_HIDDEN ** -0.5),
    }


def reference(x, mem, positions, g_mix, g_mem, w_in, conv_w, g_conv_out, g_dq, g_dk,
              lam_q1, lam_k1, lam_q2, lam_k2, g_sub, w_mem_kv, g_mq, g_mk, g_mem_out,
              w_o, g_ffn, w_gate, w_up, w_down):
    cos, sin = rope_tables(positions)
    split_idx = [int(i) for i in np.cumsum(IN_SPLITS)[:-1]]
    for l in range(DEPTH):
        h = rmsnorm(x, g_mix[l])
        proj = h @ w_in[l]
        u, c_gate, b_gate, q, k, v, q_m = jnp.split(proj, split_idx, axis=-1)

        y_conv = rmsnorm(short_gated_conv(u, c_gate, b_gate, conv_w[l]), g_conv_out[l])
        y_diff = diff_attention(q, k, v, cos, sin, g_dq[l], g_dk[l],
                                lam_q1[l], lam_k1[l], lam_q2[l], lam_k2[l],
                                g_sub[l], lambda_init(l + 1))
        kv_m = rmsnorm(mem, g_mem[l]) @ w_mem_kv[l]
        y_mem = rmsnorm(memory_cross_attention(q_m, kv_m, g_mq[l], g_mk[l]), g_mem_out[l])

        mixed = jnp.concatenate([y_conv, y_diff, y_mem], axis=-1)
        x = x + mixed @ w_o[l]

        hf = rmsnorm(x, g_ffn[l])
        x = x + (jax.nn.silu(hf @ w_gate[l]) * (hf @ w_up[l])) @ w_down[l]
    return x
```

```python
import contextlib
import numpy as np
import concourse.bass as bass
import concourse.mybir as mybir
from concourse.bass_utils import run_bass_kernel_spmd

F32 = mybir.dt.float32
BF16 = mybir.dt.bfloat16
I32 = mybir.dt.int32
AF = mybir.ActivationFunctionType
ALU = mybir.AluOpType
AX = mybir.AxisListType

ENGS = ("pe", "act", "dve", "pool", "sp")
S = 2048
D = 2048
NT = 16
KC = 16
HID = 5632
NHC = 44
NU = 4
UC = 11
EPS = 1e-6
LAM_INIT = 0.2
NCST = 1731
USE_QK512 = True


class Buf:
    __slots__ = ("name", "w", "r")

    def __init__(self, name):
        self.name = name
        self.w = None
        self.r = {}


class Op:
    __slots__ = ("eng", "fn", "waits", "signal", "tick", "dma", "pos")

    def __init__(self, eng, fn):
        self.eng = eng
        self.fn = fn
        self.waits = []
        self.signal = False
        self.tick = None
        self.dma = None
        self.pos = None


class Prog:
    def __init__(self, nc):
        self.nc = nc
        self.streams = {e: [] for e in ENGS}
        self.waited = {e: {} for e in ENGS}
        self.dma_count = {}
        self.final_tokens = []

    def _add_wait(self, o, eng, t):
        wd = self.waited[eng]
        if t[0] == "c":
            if t[1] == eng and eng == "pe":
                return
            if wd.get(t[1], -1) >= t[2]:
                return
            wd[t[1]] = t[2]
            self.streams[t[1]][t[2]].signal = True
            o.waits.append(t)
        else:
            if wd.get(("d", t[1]), -1) >= t[2]:
                return
            wd[("d", t[1])] = t[2]
            o.waits.append(t)

    def op(self, eng, fn, reads=(), writes=(), dma_key=None, final=False):
        o = Op(eng, fn)
        st = self.streams[eng]
        o.pos = len(st)
        if dma_key is not None:
            v = self.dma_count.get(dma_key, 0) + 16
            self.dma_count[dma_key] = v
            o.dma = (dma_key, v)
            tok = ("d", dma_key, v)
        else:
            tok = ("c", eng, o.pos)
        for b in reads:
            if b.w is not None:
                self._add_wait(o, eng, b.w)
        for b in writes:
            if b.w is not None:
                self._add_wait(o, eng, b.w)
            for t in b.r.values():
                self._add_wait(o, eng, t)
        for b in reads:
            if tok[0] == "c":
                b.r[eng] = tok
            else:
                b.r[("d", dma_key)] = tok
        for b in writes:
            b.w = tok
            b.r = {}
        st.append(o)
        if final:
            self.final_tokens.append(tok)
        return tok

    def barrier(self):
        last = {}
        for e in ENGS:
            st = self.streams[e]
            for i in range(len(st) - 1, -1, -1):
                if st[i].fn is not None and st[i].dma is None:
                    last[e] = ("c", e, i)
                    break
        dtoks = [("d", k, v) for k, v in self.dma_count.items()]
        for e in ENGS:
            o = Op(e, None)
            o.pos = len(self.streams[e])
            for f, t in last.items():
                if f != e:
                    self._add_wait(o, e, t)
            for t in dtoks:
                self._add_wait(o, e, t)
            self.streams[e].append(o)

    def emit(self):
        nc = self.nc
        fin_waits = list(self.final_tokens)
        for t in fin_waits:
            if t[0] == "c":
                self.streams[t[1]][t[2]].signal = True
        for e in ENGS:
            c = 0
            for o in self.streams[e]:
                if o.signal:
                    c += 1
                    o.tick = c
        with contextlib.ExitStack() as es:
            esem = {e: es.enter_context(nc.semaphore("s_" + e)) for e in ENGS}
            dsem = {k: es.enter_context(nc.semaphore("d_%s" % (k,))) for k in self.dma_count}
            block = es.enter_context(nc.Block())
            streams = self.streams

            def run(ename, eng):
                for o in streams[ename]:
                    for t in o.waits:
                        if t[0] == "c":
                            eng.wait_ge(esem[t[1]], streams[t[1]][t[2]].tick)
                        else:
                            eng.wait_ge(dsem[t[1]], t[2])
                    if o.fn is None:
                        continue
                    ins = o.fn(eng)
                    if o.dma is not None:
                        ins.then_inc(dsem[o.dma[0]], 16)
                    elif o.signal:
                        ins.then_inc(esem[ename], 1)
                if ename == "sp":
                    for t in fin_waits:
                        if t[0] == "d":
                            eng.wait_ge(dsem[t[1]], t[2])
                        else:
                            eng.wait_ge(esem[t[1]], streams[t[1]][t[2]].tick)

            @block.tensor
            def _(eng):
                run("pe", eng)

            @block.scalar
            def _(eng):
                run("act", eng)

            @block.vector
            def _(eng):
                run("dve", eng)

            @block.gpsimd
            def _(eng):
                run("pool", eng)

            @block.sync
            def _(eng):
                run("sp", eng)


class Arena:
    def __init__(self, ap, n):
        self.ap, self.n, self.off = ap, n, 0

    def alloc(self, nelem, dtype=BF16):
        mult = 2 if dtype in (F32, I32) else 1
        self.off = (self.off + 15) // 16 * 16
        need = nelem * mult
        v = self.ap[:, self.off:self.off + need]
        self.off += need
        assert self.off <= self.n, ("SBUF arena overflow", self.off, self.n)
        return v.bitcast(dtype) if dtype != BF16 else v


class SlotPool:
    def __init__(self, P, name, slots, loads):
        self.P = P
        self.name = name
        self.slots = slots
        self.bufs = [Buf("%s%d" % (name, i)) for i in range(len(slots))]
        self.loads = loads
        self.next = 0
        self.free = list(range(len(slots)))
        self.where = {}
        self.pump()

    def pump(self):
        while self.free and self.next < len(self.loads):
            si = self.free.pop(0)
            key, fn = self.loads[self.next]
            self.next += 1
            o_ap, i_ap = fn(self.slots[si])
            self.P.op("pool", lambda e, o_ap=o_ap, i_ap=i_ap: e.dma_start(out=o_ap, in_=i_ap),
                      writes=[self.bufs[si]], dma_key="%s%d" % (self.name, si))
            self.where[key] = si

    def get(self, key):
        si = self.where[key]
        return self.slots[si], self.bufs[si]

    def release(self, key):
        si = self.where.pop(key)
        self.free.append(si)
        self.pump()


def build(debug=False):
    nc = bass.Bass("TRN2", target_bir_lowering=False)
    dt = nc.dram_tensor
    x_d = dt("x", [S, D], F32, kind="ExternalInput").ap()
    mem_d = dt("mem", [256, D], F32, kind="ExternalInput").ap()
    pos_d = dt("pos", [128, 16], I32, kind="ExternalInput").ap()
    cst_d = dt("cst", [128, NCST], F32, kind="ExternalInput").ap()
    wmix_d = dt("wmix", [32, 128, 4096], F32, kind="ExternalInput").ap()
    wgu_d = dt("wgu", [88, 128, 2048], F32, kind="ExternalInput").ap()
    wd_d = dt("wd", [16, 128, 5632], F32, kind="ExternalInput").ap()
    out_d = dt("out", [S, D], F32, kind="ExternalOutput").ap()
    run_d = dt("run", [S, D], F32, kind="Internal").ap()
    if debug:
        dbg_hT = dt("dbg_hT", [128, KC * S], BF16, kind="ExternalOutput").ap()
        dbg_mixA = dt("dbg_mixA", [128, 8 * S], BF16, kind="ExternalOutput").ap()
        dbg_mixB = dt("dbg_mixB", [128, 8 * S], BF16, kind="ExternalOutput").ap()
        dbg_x1 = dt("dbg_x1", [S, D], F32, kind="ExternalOutput").ap()
        dbg_hfT = dt("dbg_hfT", [128, KC * S], BF16, kind="ExternalOutput").ap()
        dbg_qkt = dt("dbg_qkt", [128, 4 * S], BF16, kind="ExternalOutput").ap()
        dbg_vx = dt("dbg_vx", [128, NT * 2 * 129], BF16, kind="ExternalOutput").ap()
        dbg_kmt = dt("dbg_kmt", [128, 1024], BF16, kind="ExternalOutput").ap()
        dbg_kraw = dt("dbg_kraw", [128, 256], F32, kind="ExternalOutput").ap()
        dbg_ssq = dt("dbg_ssq", [128, 300], F32, kind="ExternalOutput").ap()
        dbg_rim = dt("dbg_rim", [128, 256], F32, kind="ExternalOutput").ap()
        dbg_memT = dt("dbg_memT", [128, KC * 256], BF16, kind="ExternalOutput").ap()
        dbg_vm = dt("dbg_vm", [128, 1032], BF16, kind="ExternalOutput").ap()
        dbg_qmt = dt("dbg_qmt", [128, 4 * S], BF16, kind="ExternalOutput").ap()

    with contextlib.ExitStack() as es:
        NAR = 104512
        arena_t = es.enter_context(nc.sbuf_tensor("arena", [128, NAR], BF16))
        ps = es.enter_context(nc.psum_tensor("ps", [128, 4096], F32))
        AR = Arena(arena_t[:, :], NAR)
        P = Prog(nc)

        def bank(i, n=512, o=0):
            return ps[:, 512 * i + o:512 * i + o + n]

        def bankb(i):
            return ps[:, 512 * i:512 * i + 512].bitcast(BF16)

        PB = [Buf("ps%d" % i) for i in range(8)]

        def dve(fn, reads=(), writes=()):
            return P.op("dve", fn, reads, writes)

        def act(fn, reads=(), writes=()):
            return P.op("act", fn, reads, writes)

        def pe(fn, reads=(), writes=()):
            return P.op("pe", fn, reads, writes)

        cst = AR.alloc(NCST, F32)
        bcst = Buf("cst")
        P.op("sp", lambda e: e.dma_start(out=cst, in_=cst_d), writes=[bcst], dma_key="cst")
        c_ident = cst[:, 0:128]
        c_mask = cst[:, 128:256]
        c_gmix = cst[:, 256:272]
        c_gmem = cst[:, 272:288]
        c_gffn = cst[:, 288:304]
        c_convw = cst[:, 304:316].rearrange("p (j k) -> p j k", j=4)
        c_gconv = cst[:, 316:320]
        c_gmq = cst[:, 320:321]
        c_gmk = cst[:, 321:322]
        c_gqk = cst[:, 322:834]
        c_gsub = cst[:, 834:962]
        c_gmo = cst[:, 962:1474]
        c_lam = cst[:, 1474:1730]
        c_gsubf = cst[:, 1730:1731]
        identb = AR.alloc(128)
        maskb = AR.alloc(128)
        onesb = AR.alloc(128)
        masknb = AR.alloc(128)
        epsb = AR.alloc(1, F32)
        gsub08 = AR.alloc(128, F32)
        gsub08f = AR.alloc(1, F32)
        lamt = AR.alloc(8, F32)
        lamtmp = AR.alloc(64, F32)
        cosT = AR.alloc(128, F32)
        sinT = AR.alloc(128, F32)
        posi = AR.alloc(16, I32)
        posf = AR.alloc(16, F32)
        rtf = AR.alloc(128, F32)
        rti = AR.alloc(128, I32)
        rmk = AR.alloc(128, F32)
        KmT = AR.alloc(4 * 256)
        Vm = AR.alloc(2 * 4 * 129 + 8)
        bsm = Buf("small")
        blam = Buf("lam")
        brope = Buf("rope")
        bKmT = Buf("KmT")
        bVm = Buf("Vm")

        dve(lambda e: e.tensor_copy(identb, c_ident), [bcst], [bsm])
        dve(lambda e: e.tensor_copy(maskb, c_mask), [bcst], [bsm])
        dve(lambda e: e.memset(onesb, 1.0), [], [bsm])
        dve(lambda e: e.tensor_scalar(out=masknb, in0=c_mask, scalar1=-1.0, scalar2=30000.0, op0=ALU.add, op1=ALU.mult), [bcst], [bsm])
        dve(lambda e: e.memset(epsb, EPS), [], [bsm])
        dve(lambda e: e.tensor_scalar(out=gsub08, in0=c_gsub, scalar1=1.0 - LAM_INIT, scalar2=None, op0=ALU.mult), [bcst], [bsm])
        dve(lambda e: e.memset(Vm, 1.0), [], [bVm])
        dve(lambda e: e.tensor_scalar(out=gsub08f, in0=c_gsubf, scalar1=1.0 - LAM_INIT, scalar2=None, op0=ALU.mult), [bcst], [bsm])
        for i in range(2):
            dve(lambda e, i=i: e.tensor_tensor(out=lamtmp, in0=c_lam[:, 128 * i:128 * i + 64], in1=c_lam[:, 128 * i + 64:128 * i + 128], op=ALU.mult), [bcst, blam], [blam])
            dve(lambda e, i=i: e.reduce_sum(out=lamt[:, i:i + 1], in_=lamtmp, axis=AX.X), [blam], [blam])
        act(lambda e: e.activation(out=lamt[:, 0:2], in_=lamt[:, 0:2], func=AF.Exp), [blam], [blam])
        dve(lambda e: e.tensor_tensor(out=lamt[:, 2:3], in0=lamt[:, 1:2], in1=lamt[:, 0:1], op=ALU.subtract), [blam], [blam])
        dve(lambda e: e.tensor_scalar(out=lamt[:, 2:3], in0=lamt[:, 2:3], scalar1=-LAM_INIT, scalar2=None, op0=ALU.add), [blam], [blam])
        neglam = lamt[:, 2:3]
        P.op("sp", lambda e: e.dma_start(out=posi, in_=pos_d), writes=[brope], dma_key="pos")
        dve(lambda e: e.tensor_copy(posf, posi), [brope], [brope])
        TWO_PI = float(2 * np.pi)
        for tab, shift in ((sinT, 0.0), (cosT, float(np.pi / 2))):
            t3 = tab.rearrange("p (t f) -> p t f", t=16)
            for i in range(8):
                fr = float(np.float32(500000.0) ** (-np.float32(2 * i) / np.float32(16)))
                dve(lambda e, i=i, fr=fr, t3=t3: e.tensor_scalar(out=t3[:, :, i], in0=posf, scalar1=fr, scalar2=None, op0=ALU.mult), [brope], [brope])
            if shift:
                dve(lambda e, tab=tab, shift=shift: e.tensor_scalar(out=tab, in0=tab, scalar1=shift, scalar2=None, op0=ALU.add), [brope], [brope])
            dve(lambda e, tab=tab: e.tensor_scalar(out=rtf, in0=tab, scalar1=1.0 / TWO_PI, scalar2=None, op0=ALU.mult), [brope], [brope])
            dve(lambda e: e.tensor_copy(rti, rtf), [brope], [brope])
            dve(lambda e: e.tensor_copy(rtf, rti), [brope], [brope])
            dve(lambda e, tab=tab: e.scalar_tensor_tensor(out=tab, in0=rtf, scalar=-TWO_PI, in1=tab, op0=ALU.mult, op1=ALU.add), [brope], [brope])
            dve(lambda e, tab=tab: e.tensor_scalar(out=rmk, in0=tab, scalar1=float(np.pi), scalar2=None, op0=ALU.is_gt), [brope], [brope])
            dve(lambda e, tab=tab: e.scalar_tensor_tensor(out=tab, in0=rmk, scalar=-TWO_PI, in1=tab, op0=ALU.mult, op1=ALU.add), [brope], [brope])
            dve(lambda e, tab=tab: e.tensor_scalar(out=rmk, in0=tab, scalar1=float(-np.pi), scalar2=None, op0=ALU.is_lt), [brope], [brope])
            dve(lambda e, tab=tab: e.scalar_tensor_tensor(out=tab, in0=rmk, scalar=TWO_PI, in1=tab, op0=ALU.mult, op1=ALU.add), [brope], [brope])
            dve(lambda e, tab=tab: e.tensor_scalar(out=tab, in0=tab, scalar1=3.1415925, scalar2=-3.1415925, op0=ALU.min, op1=ALU.max), [brope], [brope])
            act(lambda e, tab=tab: e.activation(out=tab, in_=tab, func=AF.Sin), [brope], [brope])
        cos3 = cosT.rearrange("p (t f) -> p t f", t=16)
        sin3 = sinT.rearrange("p (t f) -> p t f", t=16)

        hT = AR.alloc(KC * S).rearrange("p (k t) -> p k t", k=KC)
        bhT = [Buf("hT%d" % t) for t in range(NT)]
        mark_ffn = AR.off
        mixT = AR.alloc(8 * S).rearrange("p (c t) -> p c t", c=8)
        bmix = [Buf("mix%d" % c) for c in range(8)]
        QKT = AR.alloc(4 * S).rearrange("p (j t) -> p j t", j=4)
        bQKT = Buf("QKT")
        Vx = AR.alloc(NT * 2 * 129 + 16)
        Vx4 = Vx[:, 0:NT * 2 * 129].rearrange("p (t h d) -> p t h d", t=NT, h=2)
        bVx = Buf("Vx")
        dve(lambda e: e.memset(Vx, 1.0), [], [bVx])
        NSLOT = 6
        slots_all = AR.alloc(4096 * NSLOT)
        wslots = [slots_all[:, i * 4096:(i + 1) * 4096] for i in range(NSLOT)]
        mark_scr = AR.off

        def wl(blk, k, n):
            def fn(slot):
                return (slot.rearrange("p (k n) -> p k n", k=k), wmix_d[blk].rearrange("p (k n) -> p k n", k=k))
            return (blk, fn)

        order = [0, 1, 2, 3, 4, 5, 6, 7, 8, 9, 10, 11, 24, 25, 26, 27, 12, 13, 14, 15, 16, 17, 18, 19, 20, 21, 22, 23, 28, 29, 30, 31]
        loads = [wl(b, 8 if b >= 24 else 16, 512 if b >= 24 else 256) for b in order]
        WP = SlotPool(P, "w", wslots, loads)

        def wv16(slot):
            return slot.rearrange("p (k n) -> p k n", k=16)

        def rsqrt_act(dst, src, n, reads, writes):
            act(lambda e: e.activation(out=dst, in_=src, func=AF.Ln, scale=1.0 / n, bias=epsb), list(reads) + [bsm], writes)
            act(lambda e: e.activation(out=dst, in_=dst, func=AF.Exp, scale=-0.5), writes, writes)

        trb = [0]

        def norm_transpose(src, bsrc, xb, bxb, st1, bst, g_fm, dstT, dcols, bdst):
            act(lambda e: e.activation(out=xb, in_=src, func=AF.Square, accum_out=st1[:, 0:1]), [bsrc], [bxb, bst])
            rsqrt_act(st1[:, 1:2], st1[:, 0:1], float(D), [bst], [bst])
            act(lambda e: e.activation(out=xb, in_=src, func=AF.Copy, scale=st1[:, 1:2]), [bsrc, bst], [bxb])
            for g in range(2):
                bi = 6 + (trb[0] % 2)
                trb[0] += 1
                pv = bankb(bi)

                def tr(e, g=g, pv=pv):
                    ins = None
                    for j in range(8):
                        kc = 8 * g + j
                        ins = e.transpose(pv[:, j * 128:(j + 1) * 128], xb[:, kc * 128:(kc + 1) * 128], identb)
                    return ins
                pe(tr, [bxb, bsm], [PB[bi]])
                dve(lambda e, g=g, pv=pv: e.tensor_tensor(out=dstT[:, 8 * g:8 * g + 8, dcols], in0=pv.rearrange("p (k t) -> p k t", k=8),
                                                          in1=g_fm[:, 8 * g:8 * g + 8].unsqueeze(2).to_broadcast([128, 8, 128]), op=ALU.mult),
                    [PB[bi], bcst], [bdst])

        xs = [AR.alloc(D, F32) for _ in range(2)]
        bxs = [Buf("xs0"), Buf("xs1")]
        xb = AR.alloc(D)
        bxb = Buf("xb")
        st1 = AR.alloc(8, F32)
        bst = Buf("st1")
        for t in range(NT):
            s = t % 2
            P.op("sp", lambda e, t=t, s=s: e.dma_start(out=xs[s], in_=x_d[t * 128:(t + 1) * 128, :]), writes=[bxs[s]], dma_key="xs%d" % s)
            norm_transpose(xs[s], bxs[s], xb, bxb, st1, bst, c_gmix, hT, slice(t * 128, (t + 1) * 128), bhT[t])

        P.barrier()
        if debug:
            P.op("sp", lambda e: e.dma_start(out=dbg_hT, in_=hT.rearrange("p k t -> p (k t)")), reads=bhT, dma_key="dbg0", final=True)
        AR.off = mark_scr
        z = AR.alloc(2064, F32)
        bz = Buf("z")
        uS = AR.alloc(512, F32)
        y1 = AR.alloc(512, F32)
        y2 = AR.alloc(512, F32)
        sqbs = [AR.alloc(512), AR.alloc(512)]
        bsqs = [Buf("sqb0"), Buf("sqb1")]
        cdef = []
        rinvc = AR.alloc(512, F32)
        bcs = Buf("convscr")
        zh = AR.alloc(8, F32).rearrange("p (j k) -> p j k", j=4)
        brc = Buf("rinvc")

        def conv_norm(tg_):
            tsl_ = slice(tg_ * 512, (tg_ + 1) * 512)
            sb_k = 6 + (tg_ % 2)
            rsqrt_act(rinvc, bank(sb_k), 512.0, [PB[sb_k]], [brc])
            for j_ in range(4):
                dve(lambda e, j_=j_: e.scalar_tensor_tensor(out=mixT[:, j_, tsl_], in0=mixT[:, j_, tsl_], scalar=c_gconv[:, j_:j_ + 1], in1=rinvc, op0=ALU.mult, op1=ALU.mult),
                    [bmix[j_], bcst, brc], [bmix[j_]])
        for tg in range(4):
            tsl = slice(tg * 512, (tg + 1) * 512)
            ssq_bank = 6 + (tg % 2)
            for j in range(4):
                set_ = (tg * 4 + j) % 2
                b_u, b_c, b_b = 3 * set_, 3 * set_ + 1, 3 * set_ + 2
                sA, bA = WP.get(j)
                sB, bB = WP.get(4 + j // 2)
                wA = wv16(sA)
                wB = wv16(sB)

                def mm(e, wA=wA, wB=wB, j=j, b_u=b_u, b_c=b_c, b_b=b_b, tsl=tsl):
                    ins = None
                    for (bk, w, c0) in ((b_u, wA, 0), (b_c, wA, 128), (b_b, wB, 128 * (j % 2))):
                        for kc in range(KC):
                            ins = e.matmul(bank(bk), lhsT=w[:, kc, c0:c0 + 128], rhs=hT[:, kc, tsl], start=(kc == 0), stop=(kc == KC - 1))
                    return ins
                pe(mm, [bA, bB] + bhT[4 * tg:4 * tg + 4], [PB[b_u], PB[b_c], PB[b_b]])
                while cdef:
                    cdef.pop(0)()
                if j == 0 and tg > 0:
                    conv_norm(tg - 1)
                zc = z[:, 16:16 + 512]
                if tg == 0:
                    dve(lambda e: e.memset(z[:, 0:16], 0.0), [bz], [bz])
                else:
                    dve(lambda e, j=j: e.tensor_copy(z[:, 14:16], zh[:, j, :]), [bz], [bz])
                act(lambda e, b_u=b_u: e.activation(out=uS, in_=bank(b_u), func=AF.Copy), [PB[b_u]], [bcs])
                dve(lambda e, b_c=b_c: e.tensor_tensor(out=zc, in0=bank(b_c), in1=uS, op=ALU.mult), [PB[b_c], bcs], [bz])
                dve(lambda e, j=j: e.tensor_scalar(out=y1, in0=zc, scalar1=c_convw[:, j, 2:3], scalar2=None, op0=ALU.mult), [bz, bcst], [bcs])
                dve(lambda e, j=j: e.scalar_tensor_tensor(out=y2, in0=z[:, 15:15 + 512], scalar=c_convw[:, j, 1:2], in1=y1, op0=ALU.mult, op1=ALU.add), [bz, bcst, bcs], [bcs])
                dve(lambda e, j=j: e.scalar_tensor_tensor(out=y1, in0=z[:, 14:14 + 512], scalar=c_convw[:, j, 0:1], in1=y2, op0=ALU.mult, op1=ALU.add), [bz, bcst, bcs], [bcs])
                dve(lambda e, b_b=b_b: e.tensor_tensor(out=y2, in0=bank(b_b), in1=y1, op=ALU.mult), [PB[b_b], bcs], [bcs])
                dve(lambda e, j=j: e.tensor_copy(zh[:, j, :], z[:, 16 + 510:16 + 512]), [bz], [bz])
                act(lambda e, j=j, tsl=tsl: e.activation(out=mixT[:, j, tsl], in_=y2, func=AF.Copy), [bcs], [bmix[j]])
                sqb, bsq = sqbs[(tg * 4 + j) % 2], bsqs[(tg * 4 + j) % 2]
                act(lambda e, sqb=sqb: e.activation(out=sqb, in_=y2, func=AF.Square), [bcs], [bsq])
                cdef.append(lambda j=j, ssq_bank=ssq_bank, sqb=sqb, bsq=bsq: pe(lambda e: e.matmul(bank(ssq_bank), lhsT=onesb, rhs=sqb, start=(j == 0), stop=(j == 3)), [bsq, bsm], [PB[ssq_bank]]))
        while cdef:
            cdef.pop(0)()
        conv_norm(3)
        for k in range(6):
            WP.release(k)

        def head_pair(hp):
            P.barrier()
            AR.off = mark_scr
            NSET = 2
            sqs = [AR.alloc(512) for _ in range(NSET)]
            qns = [AR.alloc(512, F32) for _ in range(NSET)]
            qbs = [AR.alloc(512) for _ in range(NSET)]
            st8s = [AR.alloc(16, F32) for _ in range(NSET)]
            rts = [[AR.alloc(64, F32) for _ in range(4)] for _ in range(NSET)]
            bqs = [Buf("qscr%d" % i) for i in range(NSET)]
            blk0 = 6 + 3 * hp
            sQ, bWQ = WP.get(blk0)
            sK, bWK = WP.get(blk0 + 1)
            sV, bWV = WP.get(blk0 + 2)
            wQ, wK, wV = wv16(sQ), wv16(sK), wv16(sV)
            iq, ik = WP.where[blk0], WP.where[blk0 + 1]
            wQK = None
            if USE_QK512 and ik == iq + 1:
                wQK = slots_all[:, iq * 4096:(iq + 2) * 4096].rearrange("p (s k n) -> p s k n", s=2, k=16)
            deferred = []
            for t in range(NT):
                tsl = slice(t * 128, (t + 1) * 128)
                bqk = 2 * (t % 2)
                bv = bqk + 1
                btr = 4 + (t % 2)
                si = t % NSET
                sq, qn, qb, st8, rt, bq = sqs[si], qns[si], qbs[si], st8s[si], rts[si], bqs[si]

                def mm(e, tsl=tsl, bqk=bqk, bv=bv):
                    ins = None
                    if wQK is not None:
                        for kc in range(KC):
                            ins = e.matmul(bank(bqk), lhsT=hT[:, kc, tsl], rhs=wQK[:, :, kc, :], start=(kc == 0), stop=(kc == KC - 1))
                        grp = ((wV, bv, 0),)
                    else:
                        grp = ((wQ, bqk, 0), (wK, bqk, 256), (wV, bv, 0))
                    for (w, bk, c0) in grp:
                        for kc in range(KC):
                            ins = e.matmul(bank(bk, 256, c0), lhsT=hT[:, kc, tsl], rhs=w[:, kc, :], start=(kc == 0), stop=(kc == KC - 1))
                    return ins
                pe(mm, [bWQ, bWK, bWV, bhT[t]], [PB[bqk], PB[bv]])
                while deferred:
                    deferred.pop(0)()
                act(lambda e, bv=bv, t=t: e.activation(out=Vx4[:, t, :, 0:128], in_=bank(bv, 256).rearrange("p (h d) -> p h d", h=2), func=AF.Copy), [PB[bv]], [bVx])
                act(lambda e, bqk=bqk, sq=sq: e.activation(out=sq, in_=bank(bqk), func=AF.Square), [PB[bqk]], [bq])
                dve(lambda e, sq=sq, st8=st8: e.reduce_sum(out=st8[:, 0:8], in_=sq.rearrange("p (g d) -> p g d", g=8), axis=AX.X), [bq], [bq])
                rsqrt_act(st8[:, 8:16], st8[:, 0:8], 64.0, [bq], [bq])
                dve(lambda e, bqk=bqk, qn=qn, st8=st8: e.tensor_tensor(out=qn.rearrange("p (g d) -> p g d", g=8), in0=bank(bqk).rearrange("p (g d) -> p g d", g=8),
                                                                       in1=st8[:, 8:16].unsqueeze(2).to_broadcast([128, 8, 64]), op=ALU.mult), [PB[bqk], bq], [bq])
                dve(lambda e, qn=qn: e.tensor_tensor(out=qn, in0=qn, in1=c_gqk, op=ALU.mult), [bq, bcst], [bq])
                brr = [Buf("rr%d" % k_) for k_ in range(4)]
                bqb = Buf("qb")
                dve(lambda e, qn=qn, qb=qb: e.tensor_copy(qb, qn), [bq], [bqb])
                qg3 = qn.rearrange("p (g d) -> p g d", g=8)
                qb3 = qb.rearrange("p (g d) -> p g d", g=8)
                r1_, r2_ = qg3[:, :, 0:8], qg3[:, :, 8:16]
                cb = cos3[:, t, :].unsqueeze(1).to_broadcast([128, 8, 8])
                sb_ = sin3[:, t, :].unsqueeze(1).to_broadcast([128, 8, 8])
                t4 = [r.rearrange("p (g d) -> p g d", g=8) for r in rt]
                dve(lambda e, cb=cb, t4=t4, r1_=r1_: e.tensor_tensor(out=t4[0], in0=r1_, in1=cb, op=ALU.mult), [bq, brope], [brr[0]])
                dve(lambda e, sb_=sb_, t4=t4, r2_=r2_: e.tensor_tensor(out=t4[1], in0=r2_, in1=sb_, op=ALU.mult), [bq, brope], [brr[1]])
                dve(lambda e, cb=cb, t4=t4, r2_=r2_: e.tensor_tensor(out=t4[2], in0=r2_, in1=cb, op=ALU.mult), [bq, brope], [brr[2]])
                dve(lambda e, sb_=sb_, t4=t4, r1_=r1_: e.tensor_tensor(out=t4[3], in0=r1_, in1=sb_, op=ALU.mult), [bq, brope], [brr[3]])
                dve(lambda e, t4=t4, qb3=qb3: e.tensor_tensor(out=qb3[:, :, 0:8], in0=t4[0], in1=t4[1], op=ALU.subtract), [brr[0], brr[1], bqb], [bqb])
                dve(lambda e, t4=t4, qb3=qb3: e.tensor_tensor(out=qb3[:, :, 8:16], in0=t4[2], in1=t4[3], op=ALU.add), [brr[2], brr[3], bqb], [bqb])
                pvb = bankb(btr)

                def tr(e, pvb=pvb, qb=qb):
                    ins = None
                    for jj in range(4):
                        ins = e.transpose(pvb[:, jj * 128:(jj + 1) * 128], qb[:, jj * 128:(jj + 1) * 128], identb)
                    return ins
                def fin(tr=tr, bq=bqb, btr=btr, pvb=pvb, tsl=tsl):
                    pe(tr, [bq, bsm], [PB[btr]])
                    act(lambda e: e.activation(out=QKT[:, :, tsl], in_=pvb[:, 0:512].rearrange("p (j t) -> p j t", j=4), func=AF.Copy), [PB[btr]], [bQKT])
                deferred.append(fin)
            while deferred:
                deferred.pop(0)()
            WP.release(blk0)
            WP.release(blk0 + 1)
            WP.release(blk0 + 2)
            P.barrier()
            AR.off = mark_scr
            PT = [AR.alloc(1024) for _ in range(2)]
            bPT = [Buf("PT0"), Buf("PT1")]
            c4 = AR.alloc(512, F32)
            c5 = AR.alloc(512, F32)
            c6 = AR.alloc(512, F32)
            c7 = AR.alloc(512, F32)
            sqo = AR.alloc(512)
            bc = [Buf("c4"), Buf("c5"), Buf("c6"), Buf("c7")]
            bsqo = Buf("sqo")
            SC = 0.125
            pending = [None]
            for hh in range(2):
                h = 2 * hp + hh
                mch = (4 + h) % 8
                for g in range(4):
                    nkt = 4 * g + 4
                    qks = []
                    for kt in range(nkt):
                        n0 = max(0, kt - 4 * g)
                        w = (4 - n0) * 128
                        coff = n0 * 128
                        sp_ = 2 * (kt % 2)
                        pt = PT[kt % 2]
                        bpt = bPT[kt % 2]
                        ksl = slice(kt * 128, (kt + 1) * 128)
                        qsl = slice(g * 512 + coff, (g + 1) * 512)

                        def qk(e, sp_=sp_, ksl=ksl, qsl=qsl, w=w, hh=hh, diag=(kt >= 4 * g)):
                            ins = None
                            for m in range(2):
                                ins = e.matmul(bank(sp_ + m, w), lhsT=QKT[64 * m:64 * m + 64, 2 + hh, ksl], rhs=QKT[64 * m:64 * m + 64, hh, qsl], start=True, stop=not diag)
                            if diag:
                                for m in range(2):
                                    ins = e.matmul(bank(sp_ + m, 128), lhsT=identb, rhs=masknb, start=False, stop=True)
                            return ins
                        qks.append((qk, sp_))
                    pe(qks[0][0], [bQKT, bsm], [PB[qks[0][1]], PB[qks[0][1] + 1]])
                    for kt in range(nkt):
                        n0 = max(0, kt - 4 * g)
                        w = (4 - n0) * 128
                        coff = n0 * 128
                        sp_ = 2 * (kt % 2)
                        pt = PT[kt % 2]
                        bpt = bPT[kt % 2]
                        src = ps[:, 512 * sp_:512 * sp_ + 1024].rearrange("p (m c) -> p m c", m=2)[:, :, 0:w]
                        dst = pt.rearrange("p (m c) -> p m c", m=2)[:, :, 0:w]
                        act(lambda e, src=src, dst=dst: e.activation(out=dst, in_=src, func=AF.Exp, scale=SC), [PB[sp_], PB[sp_ + 1]], [bpt])

                        def pvm(m, pt=pt, kt=kt, hh=hh, w=w, coff=coff, first=(kt == 0), last=(kt == nkt - 1)):
                            def f(e):
                                e.matmul(bank(4 + m, w, coff), lhsT=Vx4[:, kt, hh, 0:128], rhs=pt[:, m * 512:m * 512 + w], start=first, stop=last)
                                return e.matmul(bank(6 + m, w, coff), lhsT=onesb, rhs=pt[:, m * 512:m * 512 + w], start=first, stop=last)
                            return f
                        if kt + 1 < nkt:
                            pe(qks[kt + 1][0], [bQKT, bsm], [PB[qks[kt + 1][1]], PB[qks[kt + 1][1] + 1]])
                        pe(pvm(0), [bpt, bVx, bsm], [PB[4], PB[6]])
                        pe(pvm(1), [bpt, bVx, bsm], [PB[5], PB[7]])
                        if kt == min(7, nkt - 1) and pending[0] is not None:
                            pending[0]()
                            pending[0] = None
                    for cb_, bk_, bb_ in ((c4, 4, bc[0]), (c6, 6, bc[2]), (c5, 5, bc[1]), (c7, 7, bc[3])):
                        dve(lambda e, cb_=cb_, bk_=bk_: e.tensor_copy(cb_, bank(bk_)), [PB[bk_]], [bb_])

                    if True:
                        dve(lambda e: e.tensor_tensor(out=c4, in0=c4, in1=c7, op=ALU.mult), [bc[0], bc[3]], [bc[0]])
                        dve(lambda e: e.tensor_tensor(out=c5, in0=c5, in1=c6, op=ALU.mult), [bc[1], bc[2]], [bc[1]])
                        dve(lambda e: e.tensor_tensor(out=c6, in0=c6, in1=c7, op=ALU.mult), [bc[2], bc[3]], [bc[2]])
                        dve(lambda e: e.scalar_tensor_tensor(out=c4, in0=c5, scalar=neglam, in1=c4, op0=ALU.mult, op1=ALU.add), [bc[0], bc[1], blam], [bc[0]])
                        dve(lambda e: e.reciprocal(c7, c6), [bc[2]], [bc[3]])
                        dve(lambda e: e.tensor_tensor(out=c4, in0=c4, in1=c7, op=ALU.mult), [bc[0], bc[3]], [bc[0]])

                    def ep2(mch=mch, g=g):
                        act(lambda e: e.activation(out=sqo, in_=c4, func=AF.Square), [bc[0]], [bsqo])
                        pe(lambda e: e.matmul(bank(2), lhsT=onesb, rhs=sqo, start=True, stop=True), [bsqo, bsm], [PB[2]])
                        rsqrt_act(c6, bank(2), 128.0, [PB[2]], [bc[2]])
                        dve(lambda e: e.scalar_tensor_tensor(out=mixT[:, mch, g * 512:(g + 1) * 512], in0=c4, scalar=gsub08f, in1=c6, op0=ALU.mult, op1=ALU.mult),
                            [bc[0], bc[2], bsm], [bmix[mch]])
                    pending[0] = ep2
            pending[0]()

        def wo_pass(half, final):
            P.barrier()
            AR.off = mark_scr
            NST = 8
            stg = [AR.alloc(512, F32) for _ in range(NST)]
            bstg = [Buf("wst%d" % i) for i in range(NST)]
            xb2 = AR.alloc(D)
            bxb2 = Buf("xb2")
            st2 = AR.alloc(8, F32)
            bst2 = Buf("st2")
            blk0 = 24 + 4 * half
            ws = [WP.get(blk0 + cg) for cg in range(4)]
            src_d = x_d if half == 0 else run_d
            steps = [(t, cg) for t in range(NT) for cg in range(4)]
            PF = 4

            def load(i):
                t_, cg_ = steps[i]
                s_ = i % NST
                rows_ = slice(t_ * 128, (t_ + 1) * 128)
                P.op("sp", lambda e: e.dma_start(out=stg[s_], in_=src_d[rows_, cg_ * 512:(cg_ + 1) * 512]), reads=([brun[t_][cg_]] if half else []), writes=[bstg[s_]], dma_key="ws%d" % s_)
            for i in range(PF):
                load(i)
            wdef = []
            for i, (t, cg) in enumerate(steps):
                s = i % NST
                rows = slice(t * 128, (t + 1) * 128)
                if i + PF < len(steps):
                    load(i + PF)
                bk = i % 6
                wsl, wb = ws[cg]
                w8 = wsl.rearrange("p (k n) -> p k n", k=8)

                def mm(e, bk=bk, w8=w8, rows=rows):
                    ins = None
                    for c in range(8):
                        ins = e.matmul(bank(bk), lhsT=mixT[:, c, rows], rhs=w8[:, c, :], start=(c == 0), stop=(c == 7))
                    return ins
                pe(mm, [wb] + bmix, [PB[bk]])
                if cg == 2:
                    while wdef:
                        wdef.pop(0)()
                dve(lambda e, bk=bk, s=s: e.tensor_tensor(out=stg[s], in0=bank(bk), in1=stg[s], op=ALU.add), [PB[bk], bstg[s]], [bstg[s]])
                P.op("sp", lambda e, s=s, rows=rows, cg=cg: e.dma_start(out=run_d[rows, cg * 512:(cg + 1) * 512], in_=stg[s]), reads=[bstg[s]], writes=[brun[t][cg]], dma_key="wo%d" % s)
                if final:
                    act(lambda e, s=s, cg=cg: e.activation(out=QKT[:, 0, cg * 512:(cg + 1) * 512], in_=stg[s], func=AF.Square, accum_out=st2[:, cg:cg + 1]), [bstg[s]], [bQKT, bst2])
                    if cg == 3:
                        dve(lambda e: e.reduce_sum(out=st2[:, 4:5], in_=st2[:, 0:4], axis=AX.X), [bst2], [bst2])
                        rsqrt_act(st2[:, 5:6], st2[:, 4:5], float(D), [bst2], [bst2])
                        for c in range(4):
                            sc = (i - 3 + c) % NST
                            act(lambda e, sc=sc, c=c: e.activation(out=xb2[:, c * 512:(c + 1) * 512], in_=stg[sc], func=AF.Copy, scale=st2[:, 5:6]), [bstg[sc], bst2], [bxb2])

                        def fin(t=t, rows=rows):
                            for g in range(2):
                                bi = 6 + (trb[0] % 2)
                                trb[0] += 1
                                pv_ = bankb(bi)

                                def tr(e, g=g, pv_=pv_):
                                    ins = None
                                    for jx in range(8):
                                        kc = 8 * g + jx
                                        ins = e.transpose(pv_[:, jx * 128:(jx + 1) * 128], xb2[:, kc * 128:(kc + 1) * 128], identb)
                                    return ins
                                pe(tr, [bxb2, bsm], [PB[bi]])
                                dve(lambda e, g=g, pv_=pv_: e.tensor_tensor(out=hT[:, 8 * g:8 * g + 8, rows], in0=pv_.rearrange("p (k t) -> p k t", k=8),
                                                                            in1=c_gffn[:, 8 * g:8 * g + 8].unsqueeze(2).to_broadcast([128, 8, 128]), op=ALU.mult),
                                    [PB[bi], bcst], [bhT[t]])
                        wdef.append(fin)
            while wdef:
                wdef.pop(0)()
            for cg in range(4):
                WP.release(blk0 + cg)

        brun = [[Buf("run%d_%d" % (t, c)) for c in range(4)] for t in range(NT)]

        def mem_phase():
            P.barrier()
            AR.off = mark_scr
            ms = AR.alloc(D, F32)
            bms = Buf("ms")
            mb = AR.alloc(D)
            bmb = Buf("mb")
            st3 = AR.alloc(8, F32)
            bst3 = Buf("st3")
            memT = AR.alloc(KC * 256).rearrange("p (k t) -> p k t", k=KC)
            bmemT = Buf("memT")
            sqmK = ms[:, 0:256].bitcast(BF16)
            rimK = ms[:, 512:1024]
            bsc = bms
            for t in range(2):
                P.op("sp", lambda e, t=t: e.dma_start(out=ms, in_=mem_d[t * 128:(t + 1) * 128, :]), writes=[bms], dma_key="ms")
                norm_transpose(ms, bms, mb, bmb, st3, bst3, c_gmem, memT, slice(t * 128, (t + 1) * 128), bmemT)
            for hm in range(4):
                sl, wb = WP.get(20 + hm // 2)
                w = wv16(sl)
                c0 = 128 * (hm % 2)

                def mm(e, w=w, c0=c0):
                    ins = None
                    for kc in range(KC):
                        ins = e.matmul(bank(0, 256), lhsT=w[:, kc, c0:c0 + 128], rhs=memT[:, kc, :], start=(kc == 0), stop=(kc == KC - 1))
                    return ins
                pe(mm, [wb, bmemT], [PB[0]])
                act(lambda e: e.activation(out=sqmK[:, 0:256], in_=bank(0, 256), func=AF.Square), [PB[0]], [bsc])
                pe(lambda e: e.matmul(bank(1, 256), lhsT=onesb, rhs=sqmK[:, 0:256], start=True, stop=True), [bsc, bsm], [PB[1]])
                rsqrt_act(rimK[:, 0:256], bank(1, 256), 128.0, [PB[1]], [bsc])
                dve(lambda e, hm=hm: e.scalar_tensor_tensor(out=KmT[:, hm * 256:(hm + 1) * 256], in0=bank(0, 256), scalar=c_gmk, in1=rimK[:, 0:256], op0=ALU.mult, op1=ALU.mult),
                    [PB[0], bsc, bcst], [bKmT])
            if debug:
                kraw = sqmK.bitcast(F32)
                bkr = Buf("kraw")
                dve(lambda e: e.tensor_copy(kraw, bank(0, 256)), [PB[0]], [bkr])
                P.op("sp", lambda e: e.dma_start(out=dbg_kraw, in_=kraw), reads=[bkr], dma_key="dbga", final=True)
                ssqd = ms[:, 1024:1324]
                dve(lambda e: e.tensor_copy(ssqd[:, 0:256], bank(1, 256)), [PB[1]], [bkr])
                P.op("sp", lambda e: e.dma_start(out=dbg_ssq, in_=ssqd), reads=[bkr], dma_key="dbgd", final=True)
                P.op("sp", lambda e: e.dma_start(out=dbg_rim, in_=rimK[:, 0:256]), reads=[bsc], dma_key="dbgb", final=True)
                P.op("sp", lambda e: e.dma_start(out=dbg_memT, in_=memT.rearrange("p k t -> p (k t)")), reads=[bmemT], dma_key="dbgc", final=True)
            WP.release(20)
            WP.release(21)
            Vm4 = Vm[:, 0:2 * 4 * 129].rearrange("p (t h d) -> p t h d", t=2, h=4)
            for mt in range(2):
                for half in range(2):
                    sl, wb = WP.get(22 + half)
                    w = wv16(sl)

                    def mm(e, w=w, mt=mt):
                        ins = None
                        for kc in range(KC):
                            ins = e.matmul(bank(2, 256), lhsT=memT[:, kc, mt * 128:(mt + 1) * 128], rhs=w[:, kc, :], start=(kc == 0), stop=(kc == KC - 1))
                        return ins
                    pe(mm, [wb, bmemT], [PB[2]])
                    act(lambda e, mt=mt, half=half: e.activation(out=Vm4[:, mt, 2 * half:2 * half + 2, 0:128], in_=bank(2, 256).rearrange("p (h d) -> p h d", h=2), func=AF.Copy), [PB[2]], [bVm])
            WP.release(22)
            WP.release(23)
            P.barrier()
            AR.off = mark_scr
            sqm = AR.alloc(512)
            rim = AR.alloc(512, F32)
            bsc = Buf("mscr2")
            PTm = AR.alloc(1024)
            bPTm = Buf("PTm")
            ym = AR.alloc(4 * 512, F32)
            sqy = AR.alloc(512)
            ybm = AR.alloc(4 * 512)
            st5 = AR.alloc(16, F32)
            bym = Buf("ym")
            QmT = QKT
            qdef = []
            for hm in range(4):
                sl, wb = WP.get(18 + hm // 2)
                w = wv16(sl)
                c0 = 128 * (hm % 2)
                for tg in range(4):
                    tsl = slice(tg * 512, (tg + 1) * 512)
                    bk = (hm * 4 + tg) % 2

                    def mm(e, w=w, c0=c0, tsl=tsl, bk=bk):
                        ins = None
                        for kc in range(KC):
                            ins = e.matmul(bank(bk), lhsT=w[:, kc, c0:c0 + 128], rhs=hT[:, kc, tsl], start=(kc == 0), stop=(kc == KC - 1))
                        return ins
                    pe(mm, [wb] + bhT[4 * tg:4 * tg + 4], [PB[bk]])
                    while qdef:
                        qdef.pop(0)()
                    act(lambda e, bk=bk: e.activation(out=sqm, in_=bank(bk), func=AF.Square), [PB[bk]], [bsc])

                    def qfin(bk=bk, hm=hm, tsl=tsl):
                        pe(lambda e: e.matmul(bank(2 + bk), lhsT=onesb, rhs=sqm, start=True, stop=True), [bsc, bsm], [PB[2 + bk]])
                        rsqrt_act(rim, bank(2 + bk), 128.0, [PB[2 + bk]], [bsc])
                        dve(lambda e: e.scalar_tensor_tensor(out=QmT[:, hm, tsl], in0=bank(bk), scalar=c_gmq, in1=rim, op0=ALU.mult, op1=ALU.mult),
                            [PB[bk], bsc, bcst], [bQKT])
                    qdef.append(qfin)
            while qdef:
                qdef.pop(0)()
            WP.release(18)
            WP.release(19)
            SCM = float(128.0 ** -0.5)
            PTms = [PTm, rim.bitcast(BF16)]
            bPTms = [bPTm, bsc]
            ucnt = [0]

            def m_qk(tg_, hm_, u_):
                tsl_ = slice(tg_ * 512, (tg_ + 1) * 512)
                sb = 2 * (u_ % 2)

                def f(e):
                    ins = None
                    for mt in range(2):
                        ins = e.matmul(bank(sb + mt), lhsT=KmT[:, hm_ * 256 + mt * 128:hm_ * 256 + (mt + 1) * 128], rhs=QmT[:, hm_, tsl_], start=True, stop=True)
                    return ins
                pe(f, [bKmT, bQKT], [PB[sb], PB[sb + 1]])
                ptm = PTms[u_ % 2]
                act(lambda e: e.activation(out=ptm, in_=ps[:, 512 * sb:512 * sb + 1024], func=AF.Exp, scale=SCM), [PB[sb], PB[sb + 1]], [bPTms[u_ % 2]])

            def m_pv(hm_, u_):
                ab = 4 + 2 * (u_ % 2)
                ptm = PTms[u_ % 2]

                def f(e):
                    ins = None
                    for n in range(4):
                        o_ = bank(ab + n // 2, 129, 256 * (n % 2))
                        for mt in range(2):
                            ins = e.matmul(o_, lhsT=ptm[:, mt * 512 + n * 128:mt * 512 + (n + 1) * 128], rhs=Vm4[:, mt, hm_, :], start=(mt == 0), stop=(mt == 1))
                    return ins
                pe(f, [bPTms[u_ % 2], bVm], [PB[ab], PB[ab + 1]])
                acc4 = ps[:, 512 * ab:512 * ab + 1024].rearrange("p (n c) -> p n c", n=4)
                dve(lambda e: e.reciprocal(st5[:, 0:4], acc4[:, :, 128]), [PB[ab], PB[ab + 1], bym], [bym])
                dve(lambda e: e.tensor_tensor(out=ym.rearrange("p (n c) -> p n c", n=4)[:, :, hm_ * 128:(hm_ + 1) * 128], in0=acc4[:, :, 0:128],
                                              in1=st5[:, 0:4].unsqueeze(2).to_broadcast([128, 4, 128]), op=ALU.mult), [PB[ab], PB[ab + 1], bym], [bym])
            for tg in range(4):
                tsl = slice(tg * 512, (tg + 1) * 512)
                u0 = ucnt[0]
                m_qk(tg, 0, u0)
                for hm in range(4):
                    if hm + 1 < 4:
                        m_qk(tg, hm + 1, u0 + hm + 1)
                    m_pv(hm, u0 + hm)
                ucnt[0] += 4
                for n in range(4):
                    yn = ym[:, n * 512:(n + 1) * 512]
                    act(lambda e, yn=yn, n=n: e.activation(out=sqy, in_=yn, func=AF.Square, accum_out=st5[:, 4 + n:5 + n]), [bym], [bym])
                rsqrt_act(st5[:, 8:12], st5[:, 4:8], 512.0, [bym], [bym])
                for n in range(4):
                    yn = ym[:, n * 512:(n + 1) * 512]
                    dve(lambda e, yn=yn, n=n: e.scalar_tensor_tensor(out=ybm[:, n * 512:(n + 1) * 512], in0=yn, scalar=st5[:, 8 + n:9 + n], in1=c_gmo, op0=ALU.mult, op1=ALU.mult), [bym, bcst], [bym])
                for hm in range(4):
                    pvb = bankb(6 + hm % 2)

                    def tr(e, pvb=pvb, hm=hm):
                        ins = None
                        for n in range(4):
                            ins = e.transpose(pvb[:, n * 128:(n + 1) * 128], ybm[:, n * 512 + hm * 128:n * 512 + (hm + 1) * 128], identb)
                        return ins
                    pe(tr, [bym, bsm], [PB[6 + hm % 2]])
                    act(lambda e, pvb=pvb, hm=hm, tsl=tsl: e.activation(out=mixT[:, 4 + hm, tsl], in_=pvb[:, 0:512], func=AF.Copy), [PB[6 + hm % 2]], [bmix[4 + hm]])

        head_pair(0)
        head_pair(1)
        if debug:
            P.barrier()
            P.op("sp", lambda e: e.dma_start(out=dbg_mixA, in_=mixT.rearrange("p c t -> p (c t)")), reads=bmix, dma_key="dbg1", final=True)
            P.op("sp", lambda e: e.dma_start(out=dbg_qkt, in_=QKT.rearrange("p c t -> p (c t)")), reads=[bQKT], dma_key="dbg3", final=True)
            P.op("sp", lambda e: e.dma_start(out=dbg_vx, in_=Vx[:, 0:NT * 2 * 129]), reads=[bVx], dma_key="dbg4", final=True)
        wo_pass(0, False)
        head_pair(2)
        head_pair(3)
        mem_phase()
        if debug:
            P.barrier()
            P.op("sp", lambda e: e.dma_start(out=dbg_mixB, in_=mixT.rearrange("p c t -> p (c t)")), reads=bmix, dma_key="dbg2", final=True)
            P.op("sp", lambda e: e.dma_start(out=dbg_kmt, in_=KmT), reads=[bKmT], dma_key="dbg7", final=True)
            P.op("sp", lambda e: e.dma_start(out=dbg_vm, in_=Vm[:, 0:1032]), reads=[bVm], dma_key="dbg8", final=True)
            P.op("sp", lambda e: e.dma_start(out=dbg_qmt, in_=QKT.rearrange("p c t -> p (c t)")), reads=[bQKT], dma_key="dbg9", final=True)
        wo_pass(1, True)
        if debug:
            P.barrier()
            P.op("sp", lambda e: e.dma_start(out=dbg_x1, in_=run_d), reads=[b_ for bb in brun for b_ in bb], dma_key="dbg5", final=True)
            P.op("sp", lambda e: e.dma_start(out=dbg_hfT, in_=hT.rearrange("p k t -> p (k t)")), reads=bhT, dma_key="dbg6", final=True)

        P.barrier()
        AR.off = mark_ffn
        hfT = hT
        actT = AR.alloc(UC * S).rearrange("p (c t) -> p c t", c=UC)
        bactT = Buf("actT")
        gslots = [AR.alloc(2048) for _ in range(6)]
        dslots = [AR.alloc(UC * 512) for _ in range(2)]
        sg = [AR.alloc(1024) for _ in range(2)]
        bsg = [Buf("sg0"), Buf("sg1")]
        NSTG = 8
        rtf_ = [AR.alloc(512, F32) for _ in range(NSTG)]
        brtf = [Buf("rf%d" % i) for i in range(NSTG)]

        def gl(idx):
            def fn(slot):
                return (slot.rearrange("p (k n) -> p k n", k=16), wgu_d[idx].rearrange("p (k n) -> p k n", k=16))
            return (idx, fn)

        def dl(idx):
            def fn(slot):
                return (slot.rearrange("p (k n) -> p k n", k=UC), wd_d[idx].rearrange("p (k n) -> p k n", k=UC))
            return (idx, fn)
        GP = SlotPool(P, "g", gslots, [gl(i) for i in range(88)])
        DP = SlotPool(P, "dn", dslots, [dl(i) for i in range(16)])
        dcnt = [0]
        for u in range(NU):
            for ci in range(UC):
                hc = u * UC + ci
                sgw, bg = GP.get(2 * hc)
                suw, bu = GP.get(2 * hc + 1)
                wg, wu = wv16(sgw), wv16(suw)
                for tb in range(2):
                    set_ = (ci * 2 + tb) % 2
                    b0 = 4 * set_

                    def mm(e, wg=wg, wu=wu, tb=tb, b0=b0):
                        ins = None
                        for (w, bo_) in ((wg, b0), (wu, b0 + 2)):
                            for kc in range(KC):
                                for hf in range(2):
                                    tsl = slice(tb * 1024 + hf * 512, tb * 1024 + (hf + 1) * 512)
                                    ins = e.matmul(bank(bo_ + hf), lhsT=w[:, kc, :], rhs=hfT[:, kc, tsl], start=(kc == 0), stop=(kc == KC - 1))
                        return ins
                    pe(mm, [bg, bu] + bhT[8 * tb:8 * tb + 8], [PB[b0], PB[b0 + 1], PB[b0 + 2], PB[b0 + 3]])
                    s = set_
                    act(lambda e, b0=b0, s=s: e.activation(out=sg[s], in_=ps[:, 512 * b0:512 * b0 + 1024], func=AF.Silu), [PB[b0], PB[b0 + 1]], [bsg[s]])
                    dve(lambda e, b0=b0, s=s, ci=ci, tb=tb: e.tensor_tensor(out=actT[:, ci, tb * 1024:(tb + 1) * 1024], in0=ps[:, 512 * (b0 + 2):512 * (b0 + 2) + 1024], in1=sg[s], op=ALU.mult),
                        [PB[b0 + 2], PB[b0 + 3], bsg[s]], [bactT])
                GP.release(2 * hc)
                GP.release(2 * hc + 1)
            dst_d = out_d if u == NU - 1 else run_d
            steps = [(cg, t) for cg in range(4) for t in range(NT)]
            PF = 5

            def load(i):
                cg_, t_ = steps[i]
                s_ = (dcnt[0] + i) % NSTG
                rows_ = slice(t_ * 128, (t_ + 1) * 128)
                P.op("sp", lambda e, s_=s_, rows_=rows_, cg_=cg_: e.dma_start(out=rtf_[s_], in_=run_d[rows_, cg_ * 512:(cg_ + 1) * 512]), reads=[brun[t_][cg_]], writes=[brtf[s_]], dma_key="rf%d" % s_)
            for i in range(PF):
                load(i)
            for i, (cg, t) in enumerate(steps):
                if t == 0:
                    sdw, bdw = DP.get(u * 4 + cg)
                    wdv = sdw.rearrange("p (k n) -> p k n", k=UC)
                rows = slice(t * 128, (t + 1) * 128)
                s = (dcnt[0] + i) % NSTG
                bk = (dcnt[0] + i) % 8
                if i + PF < len(steps):
                    load(i + PF)

                def mm(e, bk=bk, wdv=wdv, rows=rows):
                    ins = None
                    for ci in range(UC):
                        ins = e.matmul(bank(bk), lhsT=actT[:, ci, rows], rhs=wdv[:, ci, :], start=(ci == 0), stop=(ci == UC - 1))
                    return ins
                pe(mm, [bdw, bactT], [PB[bk]])
                dve(lambda e, bk=bk, s=s: e.tensor_tensor(out=rtf_[s], in0=bank(bk), in1=rtf_[s], op=ALU.add), [PB[bk], brtf[s]], [brtf[s]])
                P.op("sp", lambda e, s=s, rows=rows, cg=cg, dst_d=dst_d: e.dma_start(out=dst_d[rows, cg * 512:(cg + 1) * 512], in_=rtf_[s]), reads=[brtf[s]], writes=[brun[t][cg]],
                     dma_key="ow%d" % s, final=(u == NU - 1))
                if t == NT - 1:
                    DP.release(u * 4 + cg)
            dcnt[0] += len(steps)
        P.emit()
    return nc


def _blkA(W, cols):
    sub = W[:, cols]
    n = sub.shape[1]
    return np.ascontiguousarray(sub.reshape(16, 128, n).transpose(1, 0, 2).reshape(128, 16 * n))


def _prep_weights(w_in, w_mem_kv, w_o, w_gate, w_up, w_down):
    r = np.arange
    blocks = []
    for j in range(4):
        blocks.append(_blkA(w_in, np.concatenate([r(j * 128, (j + 1) * 128), 512 + r(j * 128, (j + 1) * 128)])))
    for jj in range(2):
        blocks.append(_blkA(w_in, 1024 + r(jj * 256, (jj + 1) * 256)))
    for hp in range(4):
        for base in (1536, 2560, 3584):
            blocks.append(_blkA(w_in, base + r(hp * 256, (hp + 1) * 256)))
    for jj in range(2):
        blocks.append(_blkA(w_in, 4608 + r(jj * 256, (jj + 1) * 256)))
    for jj in range(4):
        blocks.append(_blkA(w_mem_kv, r(jj * 256, (jj + 1) * 256)))
    for half in range(2):
        for cg in range(4):
            sub = w_o[half * 1024:(half + 1) * 1024, cg * 512:(cg + 1) * 512]
            blocks.append(np.ascontiguousarray(sub.reshape(8, 128, 512).transpose(1, 0, 2).reshape(128, 4096)))
    wmix = np.stack(blocks, 0).astype(np.float32)
    assert wmix.shape == (32, 128, 4096)
    wgu = np.empty((88, 128, 2048), np.float32)
    for hc in range(NHC):
        wgu[2 * hc] = _blkA(w_gate, r(hc * 128, (hc + 1) * 128))
        wgu[2 * hc + 1] = _blkA(w_up, r(hc * 128, (hc + 1) * 128))
    wd = np.empty((16, 128, UC * 512), np.float32)
    for u in range(NU):
        for cg in range(4):
            sub = w_down[u * UC * 128:(u + 1) * UC * 128, cg * 512:(cg + 1) * 512]
            wd[u * 4 + cg] = sub.reshape(UC, 128, 512).transpose(1, 0, 2).reshape(128, UC * 512)
    return wmix, wgu, wd


def _prep_consts(g_mix, g_mem, conv_w, g_conv_out, g_dq, g_dk, lam_q1, lam_k1, lam_q2, lam_k2, g_sub, g_mq, g_mk, g_mem_out, g_ffn):
    c = np.zeros((128, NCST), np.float32)
    c[:, 0:128] = np.eye(128, dtype=np.float32)
    kk = np.arange(128)
    c[:, 128:256] = (kk[:, None] <= kk[None, :]).astype(np.float32)
    c[:, 256:272] = g_mix[0].reshape(16, 128).T
    c[:, 272:288] = g_mem[0].reshape(16, 128).T
    c[:, 288:304] = g_ffn[0].reshape(16, 128).T
    c[:, 304:316] = conv_w[0].reshape(3, 4, 128).transpose(2, 1, 0).reshape(128, 12)
    c[:, 316:320] = g_conv_out[0].reshape(4, 128).T
    c[:, 320] = g_mq[0]
    c[:, 321] = g_mk[0]
    c[:, 322:834] = np.concatenate([np.tile(g_dq[0], 4), np.tile(g_dk[0], 4)])[None, :]
    c[:, 834:962] = g_sub[0][None, :]
    c[:, 962:1474] = g_mem_out[0][None, :]
    c[:, 1474:1730] = np.concatenate([lam_q1[0], lam_k1[0], lam_q2[0], lam_k2[0]])[None, :]
    c[:, 1730] = g_sub[0]
    return c


_NC_CACHE = {}


def kernel(x, mem, positions, g_mix, g_mem, w_in, conv_w, g_conv_out, g_dq, g_dk,
           lam_q1, lam_k1, lam_q2, lam_k2, g_sub, w_mem_kv, g_mq, g_mk, g_mem_out,
           w_o, g_ffn, w_gate, w_up, w_down):
    f = lambda a: np.asarray(a, dtype=np.float32)
    x = f(x)
    mem = f(mem)
    positions = np.asarray(positions, dtype=np.int32)
    wmix, wgu, wd = _prep_weights(f(w_in)[0], f(w_mem_kv)[0], f(w_o)[0], f(w_gate)[0], f(w_up)[0], f(w_down)[0])
    cst = _prep_consts(f(g_mix), f(g_mem), f(conv_w), f(g_conv_out), f(g_dq), f(g_dk), f(lam_q1), f(lam_k1), f(lam_q2), f(lam_k2),
                       f(g_sub), f(g_mq), f(g_mk), f(g_mem_out), f(g_ffn))
    if "nc" not in _NC_CACHE:
        _NC_CACHE["nc"] = build()
    nc = _NC_CACHE["nc"]
    in_maps = []
    for b in range(8):
        in_maps.append({
            "x": np.ascontiguousarray(x[b]),
            "mem": np.ascontiguousarray(mem[b]),
            "pos": np.ascontiguousarray(positions[b].reshape(16, 128).T),
            "cst": cst, "wmix": wmix, "wgu": wgu, "wd": wd,
        })
    res = run_bass_kernel_spmd(nc, in_maps, core_ids=list(range(8)))
    return np.stack([r["out"] for r in res.results], 0).astype(np.float32)
```

```python
import contextlib
import numpy as np
import concourse.bass as bass
import concourse.mybir as mybir
from concourse.bass_utils import run_bass_kernel_spmd

F32 = mybir.dt.float32
BF16 = mybir.dt.bfloat16
I32 = mybir.dt.int32
AF = mybir.ActivationFunctionType
ALU = mybir.AluOpType
AX = mybir.AxisListType

ENGS = ("pe", "act", "dve", "pool", "sp")
S = 2048
D = 2048
NT = 16
KC = 16
HID = 5632
NHC = 44
NU = 4
UC = 11
EPS = 1e-6
LAM_INIT = 0.2
NCST = 1731
USE_QK512 = True


class Buf:
    __slots__ = ("name", "w", "r")

    def __init__(self, name):
        self.name = name
        self.w = None
        self.r = {}


class Op:
    __slots__ = ("eng", "fn", "waits", "signal", "tick", "dma", "pos")

    def __init__(self, eng, fn):
        self.eng = eng
        self.fn = fn
        self.waits = []
        self.signal = False
        self.tick = None
        self.dma = None
        self.pos = None


class Prog:
    def __init__(self, nc):
        self.nc = nc
        self.streams = {e: [] for e in ENGS}
        self.waited = {e: {} for e in ENGS}
        self.dma_count = {}
        self.final_tokens = []

    def _add_wait(self, o, eng, t):
        wd = self.waited[eng]
        if t[0] == "c":
            if t[1] == eng and eng == "pe":
                return
            if wd.get(t[1], -1) >= t[2]:
                return
            wd[t[1]] = t[2]
            self.streams[t[1]][t[2]].signal = True
            o.waits.append(t)
        else:
            if wd.get(("d", t[1]), -1) >= t[2]:
                return
            wd[("d", t[1])] = t[2]
            o.waits.append(t)

    def op(self, eng, fn, reads=(), writes=(), dma_key=None, final=False):
        o = Op(eng, fn)
        st = self.streams[eng]
        o.pos = len(st)
        if dma_key is not None:
            v = self.dma_count.get(dma_key, 0) + 16
            self.dma_count[dma_key] = v
            o.dma = (dma_key, v)
            tok = ("d", dma_key, v)
        else:
            tok = ("c", eng, o.pos)
        for b in reads:
            if b.w is not None:
                self._add_wait(o, eng, b.w)
        for b in writes:
            if b.w is not None:
                self._add_wait(o, eng, b.w)
            for t in b.r.values():
                self._add_wait(o, eng, t)
        for b in reads:
            if tok[0] == "c":
                b.r[eng] = tok
            else:
                b.r[("d", dma_key)] = tok
        for b in writes:
            b.w = tok
            b.r = {}
        st.append(o)
        if final:
            self.final_tokens.append(tok)
        return tok

    def barrier(self):
        last = {}
        for e in ENGS:
            st = self.streams[e]
            for i in range(len(st) - 1, -1, -1):
                if st[i].fn is not None and st[i].dma is None:
                    last[e] = ("c", e, i)
                    break
        dtoks = [("d", k, v) for k, v in self.dma_count.items()]
        for e in ENGS:
            o = Op(e, None)
            o.pos = len(self.streams[e])
            for f, t in last.items():
                if f != e:
                    self._add_wait(o, e, t)
            for t in dtoks:
                self._add_wait(o, e, t)
            self.streams[e].append(o)

    def emit(self):
        nc = self.nc
        fin_waits = list(self.final_tokens)
        for t in fin_waits:
            if t[0] == "c":
                self.streams[t[1]][t[2]].signal = True
        for e in ENGS:
            c = 0
            for o in self.streams[e]:
                if o.signal:
                    c += 1
                    o.tick = c
        with contextlib.ExitStack() as es:
            esem = {e: es.enter_context(nc.semaphore("s_" + e)) for e in ENGS}
            dsem = {k: es.enter_context(nc.semaphore("d_%s" % (k,))) for k in self.dma_count}
            block = es.enter_context(nc.Block())
            streams = self.streams

            def run(ename, eng):
                for o in streams[ename]:
                    for t in o.waits:
                        if t[0] == "c":
                            eng.wait_ge(esem[t[1]], streams[t[1]][t[2]].tick)
                        else:
                            eng.wait_ge(dsem[t[1]], t[2])
                    if o.fn is None:
                        continue
                    ins = o.fn(eng)
                    if o.dma is not None:
                        ins.then_inc(dsem[o.dma[0]], 16)
                    elif o.signal:
                        ins.then_inc(esem[ename], 1)
                if ename == "sp":
                    for t in fin_waits:
                        if t[0] == "d":
                            eng.wait_ge(dsem[t[1]], t[2])
                        else:
                            eng.wait_ge(esem[t[1]], streams[t[1]][t[2]].tick)

            @block.tensor
            def _(eng):
                run("pe", eng)

            @block.scalar
            def _(eng):
                run("act", eng)

            @block.vector
            def _(eng):
                run("dve", eng)

            @block.gpsimd
            def _(eng):
                run("pool", eng)

            @block.sync
            def _(eng):
                run("sp", eng)


class Arena:
    def __init__(self, ap, n):
        self.ap, self.n, self.off = ap, n, 0

    def alloc(self, nelem, dtype=BF16):
        mult = 2 if dtype in (F32, I32) else 1
        self.off = (self.off + 15) // 16 * 16
        need = nelem * mult
        v = self.ap[:, self.off:self.off + need]
        self.off += need
        assert self.off <= self.n, ("SBUF arena overflow", self.off, self.n)
        return v.bitcast(dtype) if dtype != BF16 else v


class SlotPool:
    def __init__(self, P, name, slots, loads):
        self.P = P
        self.name = name
        self.slots = slots
        self.bufs = [Buf("%s%d" % (name, i)) for i in range(len(slots))]
        self.loads = loads
        self.next = 0
        self.free = list(range(len(slots)))
        self.where = {}
        self.pump()

    def pump(self):
        while self.free and self.next < len(self.loads):
            si = self.free.pop(0)
            key, fn = self.loads[self.next]
            self.next += 1
            o_ap, i_ap = fn(self.slots[si])
            self.P.op("pool", lambda e, o_ap=o_ap, i_ap=i_ap: e.dma_start(out=o_ap, in_=i_ap),
                      writes=[self.bufs[si]], dma_key="%s%d" % (self.name, si))
            self.where[key] = si

    def get(self, key):
        si = self.where[key]
        return self.slots[si], self.bufs[si]

    def release(self, key):
        si = self.where.pop(key)
        self.free.append(si)
        self.pump()


def build(debug=False):
    nc = bass.Bass("TRN2", target_bir_lowering=False)
    dt = nc.dram_tensor
    x_d = dt("x", [S, D], F32, kind="ExternalInput").ap()
    mem_d = dt("mem", [256, D], F32, kind="ExternalInput").ap()
    pos_d = dt("pos", [128, 16], I32, kind="ExternalInput").ap()
    cst_d = dt("cst", [128, NCST], F32, kind="ExternalInput").ap()
    wmix_d = dt("wmix", [32, 128, 4096], F32, kind="ExternalInput").ap()
    wgu_d = dt("wgu", [88, 128, 2048], F32, kind="ExternalInput").ap()
    wd_d = dt("wd", [16, 128, 5632], F32, kind="ExternalInput").ap()
    out_d = dt("out", [S, D], F32, kind="ExternalOutput").ap()
    run_d = dt("run", [S, D], F32, kind="Internal").ap()
    if debug:
        dbg_hT = dt("dbg_hT", [128, KC * S], BF16, kind="ExternalOutput").ap()
        dbg_mixA = dt("dbg_mixA", [128, 8 * S], BF16, kind="ExternalOutput").ap()
        dbg_mixB = dt("dbg_mixB", [128, 8 * S], BF16, kind="ExternalOutput").ap()
        dbg_x1 = dt("dbg_x1", [S, D], F32, kind="ExternalOutput").ap()
        dbg_hfT = dt("dbg_hfT", [128, KC * S], BF16, kind="ExternalOutput").ap()
        dbg_qkt = dt("dbg_qkt", [128, 4 * S], BF16, kind="ExternalOutput").ap()
        dbg_vx = dt("dbg_vx", [128, NT * 2 * 129], BF16, kind="ExternalOutput").ap()
        dbg_kmt = dt("dbg_kmt", [128, 1024], BF16, kind="ExternalOutput").ap()
        dbg_kraw = dt("dbg_kraw", [128, 256], F32, kind="ExternalOutput").ap()
        dbg_ssq = dt("dbg_ssq", [128, 300], F32, kind="ExternalOutput").ap()
        dbg_rim = dt("dbg_rim", [128, 256], F32, kind="ExternalOutput").ap()
        dbg_memT = dt("dbg_memT", [128, KC * 256], BF16, kind="ExternalOutput").ap()
        dbg_vm = dt("dbg_vm", [128, 1032], BF16, kind="ExternalOutput").ap()
        dbg_qmt = dt("dbg_qmt", [128, 4 * S], BF16, kind="ExternalOutput").ap()

    with contextlib.ExitStack() as es:
        NAR = 104512
        arena_t = es.enter_context(nc.sbuf_tensor("arena", [128, NAR], BF16))
        ps = es.enter_context(nc.psum_tensor("ps", [128, 4096], F32))
        AR = Arena(arena_t[:, :], NAR)
        P = Prog(nc)

        def bank(i, n=512, o=0):
            return ps[:, 512 * i + o:512 * i + o + n]

        def bankb(i):
            return ps[:, 512 * i:512 * i + 512].bitcast(BF16)

        PB = [Buf("ps%d" % i) for i in range(8)]

        def dve(fn, reads=(), writes=()):
            return P.op("dve", fn, reads, writes)

        def act(fn, reads=(), writes=()):
            return P.op("act", fn, reads, writes)

        def pe(fn, reads=(), writes=()):
            return P.op("pe", fn, reads, writes)

        cst = AR.alloc(NCST, F32)
        bcst = Buf("cst")
        P.op("sp", lambda e: e.dma_start(out=cst, in_=cst_d), writes=[bcst], dma_key="cst")
        c_ident = cst[:, 0:128]
        c_mask = cst[:, 128:256]
        c_gmix = cst[:, 256:272]
        c_gmem = cst[:, 272:288]
        c_gffn = cst[:, 288:304]
        c_convw = cst[:, 304:316].rearrange("p (j k) -> p j k", j=4)
        c_gconv = cst[:, 316:320]
        c_gmq = cst[:, 320:321]
        c_gmk = cst[:, 321:322]
        c_gqk = cst[:, 322:834]
        c_gsub = cst[:, 834:962]
        c_gmo = cst[:, 962:1474]
        c_lam = cst[:, 1474:1730]
        c_gsubf = cst[:, 1730:1731]
        identb = AR.alloc(128)
        maskb = AR.alloc(128)
        onesb = AR.alloc(128)
        masknb = AR.alloc(128)
        epsb = AR.alloc(1, F32)
        gsub08 = AR.alloc(128, F32)
        gsub08f = AR.alloc(1, F32)
        lamt = AR.alloc(8, F32)
        lamtmp = AR.alloc(64, F32)
        cosT = AR.alloc(128, F32)
        sinT = AR.alloc(128, F32)
        posi = AR.alloc(16, I32)
        posf = AR.alloc(16, F32)
        rtf = AR.alloc(128, F32)
        rti = AR.alloc(128, I32)
        rmk = AR.alloc(128, F32)
        KmT = AR.alloc(4 * 256)
        Vm = AR.alloc(2 * 4 * 129 + 8)
        bsm = Buf("small")
        blam = Buf("lam")
        brope = Buf("rope")
        bKmT = Buf("KmT")
        bVm = Buf("Vm")

        dve(lambda e: e.tensor_copy(identb, c_ident), [bcst], [bsm])
        dve(lambda e: e.tensor_copy(maskb, c_mask), [bcst], [bsm])
        dve(lambda e: e.memset(onesb, 1.0), [], [bsm])
        dve(lambda e: e.tensor_scalar(out=masknb, in0=c_mask, scalar1=-1.0, scalar2=30000.0, op0=ALU.add, op1=ALU.mult), [bcst], [bsm])
        dve(lambda e: e.memset(epsb, EPS), [], [bsm])
        dve(lambda e: e.tensor_scalar(out=gsub08, in0=c_gsub, scalar1=1.0 - LAM_INIT, scalar2=None, op0=ALU.mult), [bcst], [bsm])
        dve(lambda e: e.memset(Vm, 1.0), [], [bVm])
        dve(lambda e: e.tensor_scalar(out=gsub08f, in0=c_gsubf, scalar1=1.0 - LAM_INIT, scalar2=None, op0=ALU.mult), [bcst], [bsm])
        for i in range(2):
            dve(lambda e, i=i: e.tensor_tensor(out=lamtmp, in0=c_lam[:, 128 * i:128 * i + 64], in1=c_lam[:, 128 * i + 64:128 * i + 128], op=ALU.mult), [bcst, blam], [blam])
            dve(lambda e, i=i: e.reduce_sum(out=lamt[:, i:i + 1], in_=lamtmp, axis=AX.X), [blam], [blam])
        act(lambda e: e.activation(out=lamt[:, 0:2], in_=lamt[:, 0:2], func=AF.Exp), [blam], [blam])
        dve(lambda e: e.tensor_tensor(out=lamt[:, 2:3], in0=lamt[:, 1:2], in1=lamt[:, 0:1], op=ALU.subtract), [blam], [blam])
        dve(lambda e: e.tensor_scalar(out=lamt[:, 2:3], in0=lamt[:, 2:3], scalar1=-LAM_INIT, scalar2=None, op0=ALU.add), [blam], [blam])
        neglam = lamt[:, 2:3]
        P.op("sp", lambda e: e.dma_start(out=posi, in_=pos_d), writes=[brope], dma_key="pos")
        dve(lambda e: e.tensor_copy(posf, posi), [brope], [brope])
        TWO_PI = float(2 * np.pi)
        for tab, shift in ((sinT, 0.0), (cosT, float(np.pi / 2))):
            t3 = tab.rearrange("p (t f) -> p t f", t=16)
            for i in range(8):
                fr = float(np.float32(500000.0) ** (-np.float32(2 * i) / np.float32(16)))
                dve(lambda e, i=i, fr=fr, t3=t3: e.tensor_scalar(out=t3[:, :, i], in0=posf, scalar1=fr, scalar2=None, op0=ALU.mult), [brope], [brope])
            if shift:
                dve(lambda e, tab=tab, shift=shift: e.tensor_scalar(out=tab, in0=tab, scalar1=shift, scalar2=None, op0=ALU.add), [brope], [brope])
            dve(lambda e, tab=tab: e.tensor_scalar(out=rtf, in0=tab, scalar1=1.0 / TWO_PI, scalar2=None, op0=ALU.mult), [brope], [brope])
            dve(lambda e: e.tensor_copy(rti, rtf), [brope], [brope])
            dve(lambda e: e.tensor_copy(rtf, rti), [brope], [brope])
            dve(lambda e, tab=tab: e.scalar_tensor_tensor(out=tab, in0=rtf, scalar=-TWO_PI, in1=tab, op0=ALU.mult, op1=ALU.add), [brope], [brope])
            dve(lambda e, tab=tab: e.tensor_scalar(out=rmk, in0=tab, scalar1=float(np.pi), scalar2=None, op0=ALU.is_gt), [brope], [brope])
            dve(lambda e, tab=tab: e.scalar_tensor_tensor(out=tab, in0=rmk, scalar=-TWO_PI, in1=tab, op0=ALU.mult, op1=ALU.add), [brope], [brope])
            dve(lambda e, tab=tab: e.tensor_scalar(out=rmk, in0=tab, scalar1=float(-np.pi), scalar2=None, op0=ALU.is_lt), [brope], [brope])
            dve(lambda e, tab=tab: e.scalar_tensor_tensor(out=tab, in0=rmk, scalar=TWO_PI, in1=tab, op0=ALU.mult, op1=ALU.add), [brope], [brope])
            dve(lambda e, tab=tab: e.tensor_scalar(out=tab, in0=tab, scalar1=3.1415925, scalar2=-3.1415925, op0=ALU.min, op1=ALU.max), [brope], [brope])
            act(lambda e, tab=tab: e.activation(out=tab, in_=tab, func=AF.Sin), [brope], [brope])
        cos3 = cosT.rearrange("p (t f) -> p t f", t=16)
        sin3 = sinT.rearrange("p (t f) -> p t f", t=16)

        hT = AR.alloc(KC * S).rearrange("p (k t) -> p k t", k=KC)
        bhT = [Buf("hT%d" % t) for t in range(NT)]
        mark_ffn = AR.off
        mixT = AR.alloc(8 * S).rearrange("p (c t) -> p c t", c=8)
        bmix = [Buf("mix%d" % c) for c in range(8)]
        QKT = AR.alloc(4 * S).rearrange("p (j t) -> p j t", j=4)
        bQKT = Buf("QKT")
        Vx = AR.alloc(NT * 2 * 129 + 16)
        Vx4 = Vx[:, 0:NT * 2 * 129].rearrange("p (t h d) -> p t h d", t=NT, h=2)
        bVx = Buf("Vx")
        dve(lambda e: e.memset(Vx, 1.0), [], [bVx])
        NSLOT = 6
        slots_all = AR.alloc(4096 * NSLOT)
        wslots = [slots_all[:, i * 4096:(i + 1) * 4096] for i in range(NSLOT)]
        mark_scr = AR.off

        def wl(blk, k, n):
            def fn(slot):
                return (slot.rearrange("p (k n) -> p k n", k=k), wmix_d[blk].rearrange("p (k n) -> p k n", k=k))
            return (blk, fn)

        order = [0, 1, 2, 3, 4, 5, 6, 7, 8, 9, 10, 11, 24, 25, 26, 27, 12, 13, 14, 15, 16, 17, 18, 19, 20, 21, 22, 23, 28, 29, 30, 31]
        loads = [wl(b, 8 if b >= 24 else 16, 512 if b >= 24 else 256) for b in order]
        WP = SlotPool(P, "w", wslots, loads)

        def wv16(slot):
            return slot.rearrange("p (k n) -> p k n", k=16)

        def rsqrt_act(dst, src, n, reads, writes):
            act(lambda e: e.activation(out=dst, in_=src, func=AF.Ln, scale=1.0 / n, bias=epsb), list(reads) + [bsm], writes)
            act(lambda e: e.activation(out=dst, in_=dst, func=AF.Exp, scale=-0.5), writes, writes)

        trb = [0]

        def norm_transpose(src, bsrc, xb, bxb, st1, bst, g_fm, dstT, dcols, bdst):
            act(lambda e: e.activation(out=xb, in_=src, func=AF.Square, accum_out=st1[:, 0:1]), [bsrc], [bxb, bst])
            rsqrt_act(st1[:, 1:2], st1[:, 0:1], float(D), [bst], [bst])
            act(lambda e: e.activation(out=xb[:, 0:D // 2], in_=src[:, 0:D // 2], func=AF.Copy, scale=st1[:, 1:2]), [bsrc, bst], [bxb])
            dve(lambda e: e.tensor_scalar(out=xb[:, D // 2:D], in0=src[:, D // 2:D], scalar1=st1[:, 1:2], scalar2=None, op0=ALU.mult), [bsrc, bst], [bxb])
            for g in range(2):
                bi = 6 + (trb[0] % 2)
                trb[0] += 1
                pv = bankb(bi)

                def tr(e, g=g, pv=pv):
                    ins = None
                    for j in range(8):
                        kc = 8 * g + j
                        ins = e.transpose(pv[:, j * 128:(j + 1) * 128], xb[:, kc * 128:(kc + 1) * 128], identb)
                    return ins
                pe(tr, [bxb, bsm], [PB[bi]])
                dve(lambda e, g=g, pv=pv: e.tensor_tensor(out=dstT[:, 8 * g:8 * g + 8, dcols], in0=pv.rearrange("p (k t) -> p k t", k=8),
                                                          in1=g_fm[:, 8 * g:8 * g + 8].unsqueeze(2).to_broadcast([128, 8, 128]), op=ALU.mult),
                    [PB[bi], bcst], [bdst])

        xs = [AR.alloc(D, F32) for _ in range(2)]
        bxs = [Buf("xs0"), Buf("xs1")]
        xb = AR.alloc(D)
        bxb = Buf("xb")
        st1 = AR.alloc(8, F32)
        bst = Buf("st1")
        for t in range(NT):
            s = t % 2
            P.op("sp", lambda e, t=t, s=s: e.dma_start(out=xs[s], in_=x_d[t * 128:(t + 1) * 128, :]), writes=[bxs[s]], dma_key="xs%d" % s)
            norm_transpose(xs[s], bxs[s], xb, bxb, st1, bst, c_gmix, hT, slice(t * 128, (t + 1) * 128), bhT[t])

        P.barrier()
        if debug:
            P.op("sp", lambda e: e.dma_start(out=dbg_hT, in_=hT.rearrange("p k t -> p (k t)")), reads=bhT, dma_key="dbg0", final=True)
        AR.off = mark_scr
        z = AR.alloc(2064, F32)
        bz = Buf("z")
        uS = AR.alloc(512, F32)
        y1 = AR.alloc(512, F32)
        y2 = AR.alloc(512, F32)
        sqbs = [AR.alloc(512), AR.alloc(512)]
        bsqs = [Buf("sqb0"), Buf("sqb1")]
        cdef = []
        rinvc = AR.alloc(512, F32)
        bcs = Buf("convscr")
        zh = AR.alloc(8, F32).rearrange("p (j k) -> p j k", j=4)
        brc = Buf("rinvc")

        def conv_norm(tg_):
            tsl_ = slice(tg_ * 512, (tg_ + 1) * 512)
            sb_k = 6 + (tg_ % 2)
            rsqrt_act(rinvc, bank(sb_k), 512.0, [PB[sb_k]], [brc])
            for j_ in range(4):
                dve(lambda e, j_=j_: e.scalar_tensor_tensor(out=mixT[:, j_, tsl_], in0=mixT[:, j_, tsl_], scalar=c_gconv[:, j_:j_ + 1], in1=rinvc, op0=ALU.mult, op1=ALU.mult),
                    [bmix[j_], bcst, brc], [bmix[j_]])
        for tg in range(4):
            tsl = slice(tg * 512, (tg + 1) * 512)
            ssq_bank = 6 + (tg % 2)
            for j in range(4):
                set_ = (tg * 4 + j) % 2
                b_u, b_c, b_b = 3 * set_, 3 * set_ + 1, 3 * set_ + 2
                sA, bA = WP.get(j)
                sB, bB = WP.get(4 + j // 2)
                wA = wv16(sA)
                wB = wv16(sB)

                def mm(e, wA=wA, wB=wB, j=j, b_u=b_u, b_c=b_c, b_b=b_b, tsl=tsl):
                    ins = None
                    for (bk, w, c0) in ((b_u, wA, 0), (b_c, wA, 128), (b_b, wB, 128 * (j % 2))):
                        for kc in range(KC):
                            ins = e.matmul(bank(bk), lhsT=w[:, kc, c0:c0 + 128], rhs=hT[:, kc, tsl], start=(kc == 0), stop=(kc == KC - 1))
                    return ins
                pe(mm, [bA, bB] + bhT[4 * tg:4 * tg + 4], [PB[b_u], PB[b_c], PB[b_b]])
                while cdef:
                    cdef.pop(0)()
                if j == 0 and tg > 0:
                    conv_norm(tg - 1)
                zc = z[:, 16:16 + 512]
                if tg == 0:
                    dve(lambda e: e.memset(z[:, 0:16], 0.0), [bz], [bz])
                else:
                    dve(lambda e, j=j: e.tensor_copy(z[:, 14:16], zh[:, j, :]), [bz], [bz])
                act(lambda e, b_u=b_u: e.activation(out=uS, in_=bank(b_u), func=AF.Copy), [PB[b_u]], [bcs])
                dve(lambda e, b_c=b_c: e.tensor_tensor(out=zc, in0=bank(b_c), in1=uS, op=ALU.mult), [PB[b_c], bcs], [bz])
                dve(lambda e, j=j: e.tensor_scalar(out=y1, in0=zc, scalar1=c_convw[:, j, 2:3], scalar2=None, op0=ALU.mult), [bz, bcst], [bcs])
                dve(lambda e, j=j: e.scalar_tensor_tensor(out=y2, in0=z[:, 15:15 + 512], scalar=c_convw[:, j, 1:2], in1=y1, op0=ALU.mult, op1=ALU.add), [bz, bcst, bcs], [bcs])
                dve(lambda e, j=j: e.scalar_tensor_tensor(out=y1, in0=z[:, 14:14 + 512], scalar=c_convw[:, j, 0:1], in1=y2, op0=ALU.mult, op1=ALU.add), [bz, bcst, bcs], [bcs])
                dve(lambda e, b_b=b_b: e.tensor_tensor(out=y2, in0=bank(b_b), in1=y1, op=ALU.mult), [PB[b_b], bcs], [bcs])
                dve(lambda e, j=j: e.tensor_copy(zh[:, j, :], z[:, 16 + 510:16 + 512]), [bz], [bz])
                act(lambda e, j=j, tsl=tsl: e.activation(out=mixT[:, j, tsl], in_=y2, func=AF.Copy), [bcs], [bmix[j]])
                sqb, bsq = sqbs[(tg * 4 + j) % 2], bsqs[(tg * 4 + j) % 2]
                act(lambda e, sqb=sqb: e.activation(out=sqb, in_=y2, func=AF.Square), [bcs], [bsq])
                cdef.append(lambda j=j, ssq_bank=ssq_bank, sqb=sqb, bsq=bsq: pe(lambda e: e.matmul(bank(ssq_bank), lhsT=onesb, rhs=sqb, start=(j == 0), stop=(j == 3)), [bsq, bsm], [PB[ssq_bank]]))
        while cdef:
            cdef.pop(0)()
        conv_norm(3)
        for k in range(6):
            WP.release(k)

        def head_pair(hp):
            P.barrier()
            AR.off = mark_scr
            NSET = 2
            sqs = [AR.alloc(512) for _ in range(NSET)]
            qns = [AR.alloc(512, F32) for _ in range(NSET)]
            qbs = [AR.alloc(512) for _ in range(NSET)]
            st8s = [AR.alloc(16, F32) for _ in range(NSET)]
            rts = [[AR.alloc(64, F32) for _ in range(4)] for _ in range(NSET)]
            bqs = [Buf("qscr%d" % i) for i in range(NSET)]
            blk0 = 6 + 3 * hp
            sQ, bWQ = WP.get(blk0)
            sK, bWK = WP.get(blk0 + 1)
            sV, bWV = WP.get(blk0 + 2)
            wQ, wK, wV = wv16(sQ), wv16(sK), wv16(sV)
            iq, ik = WP.where[blk0], WP.where[blk0 + 1]
            wQK = None
            if USE_QK512 and ik == iq + 1:
                wQK = slots_all[:, iq * 4096:(iq + 2) * 4096].rearrange("p (s k n) -> p s k n", s=2, k=16)
            deferred = []
            for t in range(NT):
                tsl = slice(t * 128, (t + 1) * 128)
                bqk = 2 * (t % 2)
                bv = bqk + 1
                btr = 4 + (t % 2)
                si = t % NSET
                sq, qn, qb, st8, rt, bq = sqs[si], qns[si], qbs[si], st8s[si], rts[si], bqs[si]

                def mm(e, tsl=tsl, bqk=bqk, bv=bv):
                    ins = None
                    if wQK is not None:
                        for kc in range(KC):
                            ins = e.matmul(bank(bqk), lhsT=hT[:, kc, tsl], rhs=wQK[:, :, kc, :], start=(kc == 0), stop=(kc == KC - 1))
                        grp = ((wV, bv, 0),)
                    else:
                        grp = ((wQ, bqk, 0), (wK, bqk, 256), (wV, bv, 0))
                    for (w, bk, c0) in grp:
                        for kc in range(KC):
                            ins = e.matmul(bank(bk, 256, c0), lhsT=hT[:, kc, tsl], rhs=w[:, kc, :], start=(kc == 0), stop=(kc == KC - 1))
                    return ins
                pe(mm, [bWQ, bWK, bWV, bhT[t]], [PB[bqk], PB[bv]])
                while deferred:
                    deferred.pop(0)()
                act(lambda e, bv=bv, t=t: e.activation(out=Vx4[:, t, :, 0:128], in_=bank(bv, 256).rearrange("p (h d) -> p h d", h=2), func=AF.Copy), [PB[bv]], [bVx])
                act(lambda e, bqk=bqk, sq=sq: e.activation(out=sq, in_=bank(bqk), func=AF.Square), [PB[bqk]], [bq])
                dve(lambda e, sq=sq, st8=st8: e.reduce_sum(out=st8[:, 0:8], in_=sq.rearrange("p (g d) -> p g d", g=8), axis=AX.X), [bq], [bq])
                rsqrt_act(st8[:, 8:16], st8[:, 0:8], 64.0, [bq], [bq])
                dve(lambda e, bqk=bqk, qn=qn, st8=st8: e.tensor_tensor(out=qn.rearrange("p (g d) -> p g d", g=8), in0=bank(bqk).rearrange("p (g d) -> p g d", g=8),
                                                                       in1=st8[:, 8:16].unsqueeze(2).to_broadcast([128, 8, 64]), op=ALU.mult), [PB[bqk], bq], [bq])
                dve(lambda e, qn=qn: e.tensor_tensor(out=qn, in0=qn, in1=c_gqk, op=ALU.mult), [bq, bcst], [bq])
                brr = [Buf("rr%d" % k_) for k_ in range(4)]
                bqb = Buf("qb")
                dve(lambda e, qn=qn, qb=qb: e.tensor_copy(qb, qn), [bq], [bqb])
                qg3 = qn.rearrange("p (g d) -> p g d", g=8)
                qb3 = qb.rearrange("p (g d) -> p g d", g=8)
                r1_, r2_ = qg3[:, :, 0:8], qg3[:, :, 8:16]
                cb = cos3[:, t, :].unsqueeze(1).to_broadcast([128, 8, 8])
                sb_ = sin3[:, t, :].unsqueeze(1).to_broadcast([128, 8, 8])
                t4 = [r.rearrange("p (g d) -> p g d", g=8) for r in rt]
                dve(lambda e, cb=cb, t4=t4, r1_=r1_: e.tensor_tensor(out=t4[0], in0=r1_, in1=cb, op=ALU.mult), [bq, brope], [brr[0]])
                dve(lambda e, sb_=sb_, t4=t4, r2_=r2_: e.tensor_tensor(out=t4[1], in0=r2_, in1=sb_, op=ALU.mult), [bq, brope], [brr[1]])
                dve(lambda e, cb=cb, t4=t4, r2_=r2_: e.tensor_tensor(out=t4[2], in0=r2_, in1=cb, op=ALU.mult), [bq, brope], [brr[2]])
                dve(lambda e, sb_=sb_, t4=t4, r1_=r1_: e.tensor_tensor(out=t4[3], in0=r1_, in1=sb_, op=ALU.mult), [bq, brope], [brr[3]])
                dve(lambda e, t4=t4, qb3=qb3: e.tensor_tensor(out=qb3[:, :, 0:8], in0=t4[0], in1=t4[1], op=ALU.subtract), [brr[0], brr[1], bqb], [bqb])
                dve(lambda e, t4=t4, qb3=qb3: e.tensor_tensor(out=qb3[:, :, 8:16], in0=t4[2], in1=t4[3], op=ALU.add), [brr[2], brr[3], bqb], [bqb])
                pvb = bankb(btr)

                def tr(e, pvb=pvb, qb=qb):
                    ins = None
                    for jj in range(4):
                        ins = e.transpose(pvb[:, jj * 128:(jj + 1) * 128], qb[:, jj * 128:(jj + 1) * 128], identb)
                    return ins
                def fin(tr=tr, bq=bqb, btr=btr, pvb=pvb, tsl=tsl):
                    pe(tr, [bq, bsm], [PB[btr]])
                    act(lambda e: e.activation(out=QKT[:, :, tsl], in_=pvb[:, 0:512].rearrange("p (j t) -> p j t", j=4), func=AF.Copy), [PB[btr]], [bQKT])
                deferred.append(fin)
            while deferred:
                deferred.pop(0)()
            WP.release(blk0)
            WP.release(blk0 + 1)
            WP.release(blk0 + 2)
            P.barrier()
            AR.off = mark_scr
            PT = [AR.alloc(1024) for _ in range(2)]
            bPT = [Buf("PT0"), Buf("PT1")]
            c4 = AR.alloc(512, F32)
            c5 = AR.alloc(512, F32)
            c6 = AR.alloc(512, F32)
            c7 = AR.alloc(512, F32)
            sqo = AR.alloc(512)
            bc = [Buf("c4"), Buf("c5"), Buf("c6"), Buf("c7")]
            bsqo = Buf("sqo")
            SC = 0.125
            pending = [None]
            for hh in range(2):
                h = 2 * hp + hh
                mch = (4 + h) % 8
                for g in range(4):
                    nkt = 4 * g + 4
                    qks = []
                    for kt in range(nkt):
                        n0 = max(0, kt - 4 * g)
                        w = (4 - n0) * 128
                        coff = n0 * 128
                        sp_ = 2 * (kt % 2)
                        pt = PT[kt % 2]
                        bpt = bPT[kt % 2]
                        ksl = slice(kt * 128, (kt + 1) * 128)
                        qsl = slice(g * 512 + coff, (g + 1) * 512)

                        def qk(e, sp_=sp_, ksl=ksl, qsl=qsl, w=w, hh=hh, diag=(kt >= 4 * g)):
                            ins = None
                            for m in range(2):
                                ins = e.matmul(bank(sp_ + m, w), lhsT=QKT[64 * m:64 * m + 64, 2 + hh, ksl], rhs=QKT[64 * m:64 * m + 64, hh, qsl], start=True, stop=not diag)
                            if diag:
                                for m in range(2):
                                    ins = e.matmul(bank(sp_ + m, 128), lhsT=identb, rhs=masknb, start=False, stop=True)
                            return ins
                        qks.append((qk, sp_))
                    pe(qks[0][0], [bQKT, bsm], [PB[qks[0][1]], PB[qks[0][1] + 1]])
                    for kt in range(nkt):
                        n0 = max(0, kt - 4 * g)
                        w = (4 - n0) * 128
                        coff = n0 * 128
                        sp_ = 2 * (kt % 2)
                        pt = PT[kt % 2]
                        bpt = bPT[kt % 2]
                        src = ps[:, 512 * sp_:512 * sp_ + 1024].rearrange("p (m c) -> p m c", m=2)[:, :, 0:w]
                        dst = pt.rearrange("p (m c) -> p m c", m=2)[:, :, 0:w]
                        act(lambda e, src=src, dst=dst: e.activation(out=dst, in_=src, func=AF.Exp, scale=SC), [PB[sp_], PB[sp_ + 1]], [bpt])

                        def pvm(m, pt=pt, kt=kt, hh=hh, w=w, coff=coff, first=(kt == 0), last=(kt == nkt - 1)):
                            def f(e):
                                e.matmul(bank(4 + m, w, coff), lhsT=Vx4[:, kt, hh, 0:128], rhs=pt[:, m * 512:m * 512 + w], start=first, stop=last)
                                return e.matmul(bank(6 + m, w, coff), lhsT=onesb, rhs=pt[:, m * 512:m * 512 + w], start=first, stop=last)
                            return f
                        if kt + 1 < nkt:
                            pe(qks[kt + 1][0], [bQKT, bsm], [PB[qks[kt + 1][1]], PB[qks[kt + 1][1] + 1]])
                        pe(pvm(0), [bpt, bVx, bsm], [PB[4], PB[6]])
                        pe(pvm(1), [bpt, bVx, bsm], [PB[5], PB[7]])
                        if kt == min(7, nkt - 1) and pending[0] is not None:
                            pending[0]()
                            pending[0] = None
                    for cb_, bk_, bb_ in ((c4, 4, bc[0]), (c6, 6, bc[2]), (c5, 5, bc[1]), (c7, 7, bc[3])):
                        dve(lambda e, cb_=cb_, bk_=bk_: e.tensor_copy(cb_, bank(bk_)), [PB[bk_]], [bb_])

                    if True:
                        dve(lambda e: e.tensor_tensor(out=c4, in0=c4, in1=c7, op=ALU.mult), [bc[0], bc[3]], [bc[0]])
                        dve(lambda e: e.tensor_tensor(out=c5, in0=c5, in1=c6, op=ALU.mult), [bc[1], bc[2]], [bc[1]])
                        dve(lambda e: e.tensor_tensor(out=c6, in0=c6, in1=c7, op=ALU.mult), [bc[2], bc[3]], [bc[2]])
                        dve(lambda e: e.scalar_tensor_tensor(out=c4, in0=c5, scalar=neglam, in1=c4, op0=ALU.mult, op1=ALU.add), [bc[0], bc[1], blam], [bc[0]])
                        dve(lambda e: e.reciprocal(c7, c6), [bc[2]], [bc[3]])
                        dve(lambda e: e.tensor_tensor(out=c4, in0=c4, in1=c7, op=ALU.mult), [bc[0], bc[3]], [bc[0]])

                    def ep2(mch=mch, g=g):
                        act(lambda e: e.activation(out=sqo, in_=c4, func=AF.Square), [bc[0]], [bsqo])
                        pe(lambda e: e.matmul(bank(2), lhsT=onesb, rhs=sqo, start=True, stop=True), [bsqo, bsm], [PB[2]])
                        rsqrt_act(c6, bank(2), 128.0, [PB[2]], [bc[2]])
                        dve(lambda e: e.scalar_tensor_tensor(out=mixT[:, mch, g * 512:(g + 1) * 512], in0=c4, scalar=gsub08f, in1=c6, op0=ALU.mult, op1=ALU.mult),
                            [bc[0], bc[2], bsm], [bmix[mch]])
                    pending[0] = ep2
            pending[0]()

        def wo_pass(half, final):
            P.barrier()
            AR.off = mark_scr
            NST = 8
            stg = [AR.alloc(512, F32) for _ in range(NST)]
            bstg = [Buf("wst%d" % i) for i in range(NST)]
            xb2 = AR.alloc(D)
            bxb2 = Buf("xb2")
            st2 = AR.alloc(8, F32)
            bst2 = Buf("st2")
            blk0 = 24 + 4 * half
            ws = [WP.get(blk0 + cg) for cg in range(4)]
            src_d = x_d if half == 0 else run_d
            steps = [(t, cg) for t in range(NT) for cg in range(4)]
            PF = 4

            def load(i):
                t_, cg_ = steps[i]
                s_ = i % NST
                rows_ = slice(t_ * 128, (t_ + 1) * 128)
                P.op("sp", lambda e: e.dma_start(out=stg[s_], in_=src_d[rows_, cg_ * 512:(cg_ + 1) * 512]), reads=([brun[t_][cg_]] if half else []), writes=[bstg[s_]], dma_key="ws%d" % s_)
            for i in range(PF):
                load(i)
            wdef = []
            for i, (t, cg) in enumerate(steps):
                s = i % NST
                rows = slice(t * 128, (t + 1) * 128)
                if i + PF < len(steps):
                    load(i + PF)
                bk = i % 6
                wsl, wb = ws[cg]
                w8 = wsl.rearrange("p (k n) -> p k n", k=8)

                def mm(e, bk=bk, w8=w8, rows=rows):
                    ins = None
                    for c in range(8):
                        ins = e.matmul(bank(bk), lhsT=mixT[:, c, rows], rhs=w8[:, c, :], start=(c == 0), stop=(c == 7))
                    return ins
                pe(mm, [wb] + bmix, [PB[bk]])
                if cg == 2:
                    while wdef:
                        wdef.pop(0)()
                dve(lambda e, bk=bk, s=s: e.tensor_tensor(out=stg[s], in0=bank(bk), in1=stg[s], op=ALU.add), [PB[bk], bstg[s]], [bstg[s]])
                P.op("sp", lambda e, s=s, rows=rows, cg=cg: e.dma_start(out=run_d[rows, cg * 512:(cg + 1) * 512], in_=stg[s]), reads=[bstg[s]], writes=[brun[t][cg]], dma_key="wo%d" % s)
                if final:
                    act(lambda e, s=s, cg=cg: e.activation(out=QKT[:, 0, cg * 512:(cg + 1) * 512], in_=stg[s], func=AF.Square, accum_out=st2[:, cg:cg + 1]), [bstg[s]], [bQKT, bst2])
                    if cg == 3:
                        dve(lambda e: e.reduce_sum(out=st2[:, 4:5], in_=st2[:, 0:4], axis=AX.X), [bst2], [bst2])
                        rsqrt_act(st2[:, 5:6], st2[:, 4:5], float(D), [bst2], [bst2])
                        for c in range(4):
                            sc = (i - 3 + c) % NST
                            if c < 2:
                                act(lambda e, sc=sc, c=c: e.activation(out=xb2[:, c * 512:(c + 1) * 512], in_=stg[sc], func=AF.Copy, scale=st2[:, 5:6]), [bstg[sc], bst2], [bxb2])
                            else:
                                dve(lambda e, sc=sc, c=c: e.tensor_scalar(out=xb2[:, c * 512:(c + 1) * 512], in0=stg[sc], scalar1=st2[:, 5:6], scalar2=None, op0=ALU.mult), [bstg[sc], bst2], [bxb2])

                        def fin(t=t, rows=rows):
                            for g in range(2):
                                bi = 6 + (trb[0] % 2)
                                trb[0] += 1
                                pv_ = bankb(bi)

                                def tr(e, g=g, pv_=pv_):
                                    ins = None
                                    for jx in range(8):
                                        kc = 8 * g + jx
                                        ins = e.transpose(pv_[:, jx * 128:(jx + 1) * 128], xb2[:, kc * 128:(kc + 1) * 128], identb)
                                    return ins
                                pe(tr, [bxb2, bsm], [PB[bi]])
                                dve(lambda e, g=g, pv_=pv_: e.tensor_tensor(out=hT[:, 8 * g:8 * g + 8, rows], in0=pv_.rearrange("p (k t) -> p k t", k=8),
                                                                            in1=c_gffn[:, 8 * g:8 * g + 8].unsqueeze(2).to_broadcast([128, 8, 128]), op=ALU.mult),
                                    [PB[bi], bcst], [bhT[t]])
                        wdef.append(fin)
            while wdef:
                wdef.pop(0)()
            for cg in range(4):
                WP.release(blk0 + cg)

        brun = [[Buf("run%d_%d" % (t, c)) for c in range(4)] for t in range(NT)]

        def mem_phase():
            P.barrier()
            AR.off = mark_scr
            ms = AR.alloc(D, F32)
            bms = Buf("ms")
            mb = AR.alloc(D)
            bmb = Buf("mb")
            st3 = AR.alloc(8, F32)
            bst3 = Buf("st3")
            memT = AR.alloc(KC * 256).rearrange("p (k t) -> p k t", k=KC)
            bmemT = Buf("memT")
            sqmK = ms[:, 0:256].bitcast(BF16)
            rimK = ms[:, 512:1024]
            bsc = bms
            for t in range(2):
                P.op("sp", lambda e, t=t: e.dma_start(out=ms, in_=mem_d[t * 128:(t + 1) * 128, :]), writes=[bms], dma_key="ms")
                norm_transpose(ms, bms, mb, bmb, st3, bst3, c_gmem, memT, slice(t * 128, (t + 1) * 128), bmemT)
            for hm in range(4):
                sl, wb = WP.get(20 + hm // 2)
                w = wv16(sl)
                c0 = 128 * (hm % 2)

                def mm(e, w=w, c0=c0):
                    ins = None
                    for kc in range(KC):
                        ins = e.matmul(bank(0, 256), lhsT=w[:, kc, c0:c0 + 128], rhs=memT[:, kc, :], start=(kc == 0), stop=(kc == KC - 1))
                    return ins
                pe(mm, [wb, bmemT], [PB[0]])
                act(lambda e: e.activation(out=sqmK[:, 0:256], in_=bank(0, 256), func=AF.Square), [PB[0]], [bsc])
                pe(lambda e: e.matmul(bank(1, 256), lhsT=onesb, rhs=sqmK[:, 0:256], start=True, stop=True), [bsc, bsm], [PB[1]])
                rsqrt_act(rimK[:, 0:256], bank(1, 256), 128.0, [PB[1]], [bsc])
                dve(lambda e, hm=hm: e.scalar_tensor_tensor(out=KmT[:, hm * 256:(hm + 1) * 256], in0=bank(0, 256), scalar=c_gmk, in1=rimK[:, 0:256], op0=ALU.mult, op1=ALU.mult),
                    [PB[0], bsc, bcst], [bKmT])
            if debug:
                kraw = sqmK.bitcast(F32)
                bkr = Buf("kraw")
                dve(lambda e: e.tensor_copy(kraw, bank(0, 256)), [PB[0]], [bkr])
                P.op("sp", lambda e: e.dma_start(out=dbg_kraw, in_=kraw), reads=[bkr], dma_key="dbga", final=True)
                ssqd = ms[:, 1024:1324]
                dve(lambda e: e.tensor_copy(ssqd[:, 0:256], bank(1, 256)), [PB[1]], [bkr])
                P.op("sp", lambda e: e.dma_start(out=dbg_ssq, in_=ssqd), reads=[bkr], dma_key="dbgd", final=True)
                P.op("sp", lambda e: e.dma_start(out=dbg_rim, in_=rimK[:, 0:256]), reads=[bsc], dma_key="dbgb", final=True)
                P.op("sp", lambda e: e.dma_start(out=dbg_memT, in_=memT.rearrange("p k t -> p (k t)")), reads=[bmemT], dma_key="dbgc", final=True)
            WP.release(20)
            WP.release(21)
            Vm4 = Vm[:, 0:2 * 4 * 129].rearrange("p (t h d) -> p t h d", t=2, h=4)
            for mt in range(2):
                for half in range(2):
                    sl, wb = WP.get(22 + half)
                    w = wv16(sl)

                    def mm(e, w=w, mt=mt):
                        ins = None
                        for kc in range(KC):
                            ins = e.matmul(bank(2, 256), lhsT=memT[:, kc, mt * 128:(mt + 1) * 128], rhs=w[:, kc, :], start=(kc == 0), stop=(kc == KC - 1))
                        return ins
                    pe(mm, [wb, bmemT], [PB[2]])
                    act(lambda e, mt=mt, half=half: e.activation(out=Vm4[:, mt, 2 * half:2 * half + 2, 0:128], in_=bank(2, 256).rearrange("p (h d) -> p h d", h=2), func=AF.Copy), [PB[2]], [bVm])
            WP.release(22)
            WP.release(23)
            P.barrier()
            AR.off = mark_scr
            sqm = AR.alloc(512)
            rim = AR.alloc(512, F32)
            bsc = Buf("mscr2")
            PTm = AR.alloc(1024)
            bPTm = Buf("PTm")
            ym = AR.alloc(4 * 512, F32)
            sqy = AR.alloc(512)
            ybm = AR.alloc(4 * 512)
            st5 = AR.alloc(16, F32)
            bym = Buf("ym")
            QmT = QKT
            qdef = []
            for hm in range(4):
                sl, wb = WP.get(18 + hm // 2)
                w = wv16(sl)
                c0 = 128 * (hm % 2)
                for tg in range(4):
                    tsl = slice(tg * 512, (tg + 1) * 512)
                    bk = (hm * 4 + tg) % 2

                    def mm(e, w=w, c0=c0, tsl=tsl, bk=bk):
                        ins = None
                        for kc in range(KC):
                            ins = e.matmul(bank(bk), lhsT=w[:, kc, c0:c0 + 128], rhs=hT[:, kc, tsl], start=(kc == 0), stop=(kc == KC - 1))
                        return ins
                    pe(mm, [wb] + bhT[4 * tg:4 * tg + 4], [PB[bk]])
                    while qdef:
                        qdef.pop(0)()
                    act(lambda e, bk=bk: e.activation(out=sqm, in_=bank(bk), func=AF.Square), [PB[bk]], [bsc])

                    def qfin(bk=bk, hm=hm, tsl=tsl):
                        pe(lambda e: e.matmul(bank(2 + bk), lhsT=onesb, rhs=sqm, start=True, stop=True), [bsc, bsm], [PB[2 + bk]])
                        rsqrt_act(rim, bank(2 + bk), 128.0, [PB[2 + bk]], [bsc])
                        dve(lambda e: e.scalar_tensor_tensor(out=QmT[:, hm, tsl], in0=bank(bk), scalar=c_gmq, in1=rim, op0=ALU.mult, op1=ALU.mult),
                            [PB[bk], bsc, bcst], [bQKT])
                    qdef.append(qfin)
            while qdef:
                qdef.pop(0)()
            WP.release(18)
            WP.release(19)
            SCM = float(128.0 ** -0.5)
            PTms = [PTm, rim.bitcast(BF16)]
            bPTms = [bPTm, bsc]
            ucnt = [0]

            def m_qk(tg_, hm_, u_):
                tsl_ = slice(tg_ * 512, (tg_ + 1) * 512)
                sb = 2 * (u_ % 2)

                def f(e):
                    ins = None
                    for mt in range(2):
                        ins = e.matmul(bank(sb + mt), lhsT=KmT[:, hm_ * 256 + mt * 128:hm_ * 256 + (mt + 1) * 128], rhs=QmT[:, hm_, tsl_], start=True, stop=True)
                    return ins
                pe(f, [bKmT, bQKT], [PB[sb], PB[sb + 1]])
                ptm = PTms[u_ % 2]
                act(lambda e: e.activation(out=ptm, in_=ps[:, 512 * sb:512 * sb + 1024], func=AF.Exp, scale=SCM), [PB[sb], PB[sb + 1]], [bPTms[u_ % 2]])

            def m_pv(hm_, u_):
                ab = 4 + 2 * (u_ % 2)
                ptm = PTms[u_ % 2]

                def f(e):
                    ins = None
                    for n in range(4):
                        o_ = bank(ab + n // 2, 129, 256 * (n % 2))
                        for mt in range(2):
                            ins = e.matmul(o_, lhsT=ptm[:, mt * 512 + n * 128:mt * 512 + (n + 1) * 128], rhs=Vm4[:, mt, hm_, :], start=(mt == 0), stop=(mt == 1))
                    return ins
                pe(f, [bPTms[u_ % 2], bVm], [PB[ab], PB[ab + 1]])
                acc4 = ps[:, 512 * ab:512 * ab + 1024].rearrange("p (n c) -> p n c", n=4)
                dve(lambda e: e.reciprocal(st5[:, 0:4], acc4[:, :, 128]), [PB[ab], PB[ab + 1], bym], [bym])
                dve(lambda e: e.tensor_tensor(out=ym.rearrange("p (n c) -> p n c", n=4)[:, :, hm_ * 128:(hm_ + 1) * 128], in0=acc4[:, :, 0:128],
                                              in1=st5[:, 0:4].unsqueeze(2).to_broadcast([128, 4, 128]), op=ALU.mult), [PB[ab], PB[ab + 1], bym], [bym])
            for tg in range(4):
                tsl = slice(tg * 512, (tg + 1) * 512)
                u0 = ucnt[0]
                m_qk(tg, 0, u0)
                for hm in range(4):
                    if hm + 1 < 4:
                        m_qk(tg, hm + 1, u0 + hm + 1)
                    m_pv(hm, u0 + hm)
                ucnt[0] += 4
                for n in range(4):
                    yn = ym[:, n * 512:(n + 1) * 512]
                    act(lambda e, yn=yn, n=n: e.activation(out=sqy, in_=yn, func=AF.Square, accum_out=st5[:, 4 + n:5 + n]), [bym], [bym])
                rsqrt_act(st5[:, 8:12], st5[:, 4:8], 512.0, [bym], [bym])
                for n in range(4):
                    yn = ym[:, n * 512:(n + 1) * 512]
                    dve(lambda e, yn=yn, n=n: e.scalar_tensor_tensor(out=ybm[:, n * 512:(n + 1) * 512], in0=yn, scalar=st5[:, 8 + n:9 + n], in1=c_gmo, op0=ALU.mult, op1=ALU.mult), [bym, bcst], [bym])
                for hm in range(4):
                    pvb = bankb(6 + hm % 2)

                    def tr(e, pvb=pvb, hm=hm):
                        ins = None
                        for n in range(4):
                            ins = e.transpose(pvb[:, n * 128:(n + 1) * 128], ybm[:, n * 512 + hm * 128:n * 512 + (hm + 1) * 128], identb)
                        return ins
                    pe(tr, [bym, bsm], [PB[6 + hm % 2]])
                    act(lambda e, pvb=pvb, hm=hm, tsl=tsl: e.activation(out=mixT[:, 4 + hm, tsl], in_=pvb[:, 0:512], func=AF.Copy), [PB[6 + hm % 2]], [bmix[4 + hm]])

        head_pair(0)
        head_pair(1)
        if debug:
            P.barrier()
            P.op("sp", lambda e: e.dma_start(out=dbg_mixA, in_=mixT.rearrange("p c t -> p (c t)")), reads=bmix, dma_key="dbg1", final=True)
            P.op("sp", lambda e: e.dma_start(out=dbg_qkt, in_=QKT.rearrange("p c t -> p (c t)")), reads=[bQKT], dma_key="dbg3", final=True)
            P.op("sp", lambda e: e.dma_start(out=dbg_vx, in_=Vx[:, 0:NT * 2 * 129]), reads=[bVx], dma_key="dbg4", final=True)
        wo_pass(0, False)
        head_pair(2)
        head_pair(3)
        mem_phase()
        if debug:
            P.barrier()
            P.op("sp", lambda e: e.dma_start(out=dbg_mixB, in_=mixT.rearrange("p c t -> p (c t)")), reads=bmix, dma_key="dbg2", final=True)
            P.op("sp", lambda e: e.dma_start(out=dbg_kmt, in_=KmT), reads=[bKmT], dma_key="dbg7", final=True)
            P.op("sp", lambda e: e.dma_start(out=dbg_vm, in_=Vm[:, 0:1032]), reads=[bVm], dma_key="dbg8", final=True)
            P.op("sp", lambda e: e.dma_start(out=dbg_qmt, in_=QKT.rearrange("p c t -> p (c t)")), reads=[bQKT], dma_key="dbg9", final=True)
        wo_pass(1, True)
        if debug:
            P.barrier()
            P.op("sp", lambda e: e.dma_start(out=dbg_x1, in_=run_d), reads=[b_ for bb in brun for b_ in bb], dma_key="dbg5", final=True)
            P.op("sp", lambda e: e.dma_start(out=dbg_hfT, in_=hT.rearrange("p k t -> p (k t)")), reads=bhT, dma_key="dbg6", final=True)

        P.barrier()
        AR.off = mark_ffn
        hfT = hT
        actT = AR.alloc(UC * S).rearrange("p (c t) -> p c t", c=UC)
        bactT = Buf("actT")
        gslots = [AR.alloc(2048) for _ in range(6)]
        dslots = [AR.alloc(UC * 512) for _ in range(2)]
        sg = [AR.alloc(1024) for _ in range(2)]
        bsg = [Buf("sg0"), Buf("sg1")]
        NSTG = 8
        rtf_ = [AR.alloc(512, F32) for _ in range(NSTG)]
        brtf = [Buf("rf%d" % i) for i in range(NSTG)]

        def gl(idx):
            def fn(slot):
                return (slot.rearrange("p (k n) -> p k n", k=16), wgu_d[idx].rearrange("p (k n) -> p k n", k=16))
            return (idx, fn)

        def dl(idx):
            def fn(slot):
                return (slot.rearrange("p (k n) -> p k n", k=UC), wd_d[idx].rearrange("p (k n) -> p k n", k=UC))
            return (idx, fn)
        GP = SlotPool(P, "g", gslots, [gl(i) for i in range(88)])
        DP = SlotPool(P, "dn", dslots, [dl(i) for i in range(16)])
        dcnt = [0]
        for u in range(NU):
            for ci in range(UC):
                hc = u * UC + ci
                sgw, bg = GP.get(2 * hc)
                suw, bu = GP.get(2 * hc + 1)
                wg, wu = wv16(sgw), wv16(suw)
                for tb in range(2):
                    set_ = (ci * 2 + tb) % 2
                    b0 = 4 * set_

                    def mm(e, wg=wg, wu=wu, tb=tb, b0=b0):
                        ins = None
                        for (w, bo_) in ((wg, b0), (wu, b0 + 2)):
                            for kc in range(KC):
                                for hf in range(2):
                                    tsl = slice(tb * 1024 + hf * 512, tb * 1024 + (hf + 1) * 512)
                                    ins = e.matmul(bank(bo_ + hf), lhsT=w[:, kc, :], rhs=hfT[:, kc, tsl], start=(kc == 0), stop=(kc == KC - 1))
                        return ins
                    pe(mm, [bg, bu] + bhT[8 * tb:8 * tb + 8], [PB[b0], PB[b0 + 1], PB[b0 + 2], PB[b0 + 3]])
                    s = set_
                    act(lambda e, b0=b0, s=s: e.activation(out=sg[s], in_=ps[:, 512 * b0:512 * b0 + 1024], func=AF.Silu), [PB[b0], PB[b0 + 1]], [bsg[s]])
                    dve(lambda e, b0=b0, s=s, ci=ci, tb=tb: e.tensor_tensor(out=actT[:, ci, tb * 1024:(tb + 1) * 1024], in0=ps[:, 512 * (b0 + 2):512 * (b0 + 2) + 1024], in1=sg[s], op=ALU.mult),
                        [PB[b0 + 2], PB[b0 + 3], bsg[s]], [bactT])
                GP.release(2 * hc)
                GP.release(2 * hc + 1)
            dst_d = out_d if u == NU - 1 else run_d
            steps = [(cg, t) for cg in range(4) for t in range(NT)]
            PF = 5

            def load(i):
                cg_, t_ = steps[i]
                s_ = (dcnt[0] + i) % NSTG
                rows_ = slice(t_ * 128, (t_ + 1) * 128)
                P.op("sp", lambda e, s_=s_, rows_=rows_, cg_=cg_: e.dma_start(out=rtf_[s_], in_=run_d[rows_, cg_ * 512:(cg_ + 1) * 512]), reads=[brun[t_][cg_]], writes=[brtf[s_]], dma_key="rf%d" % s_)
            for i in range(PF):
                load(i)
            for i, (cg, t) in enumerate(steps):
                if t == 0:
                    sdw, bdw = DP.get(u * 4 + cg)
                    wdv = sdw.rearrange("p (k n) -> p k n", k=UC)
                rows = slice(t * 128, (t + 1) * 128)
                s = (dcnt[0] + i) % NSTG
                bk = (dcnt[0] + i) % 8
                if i + PF < len(steps):
                    load(i + PF)

                def mm(e, bk=bk, wdv=wdv, rows=rows):
                    ins = None
                    for ci in range(UC):
                        ins = e.matmul(bank(bk), lhsT=actT[:, ci, rows], rhs=wdv[:, ci, :], start=(ci == 0), stop=(ci == UC - 1))
                    return ins
                pe(mm, [bdw, bactT], [PB[bk]])
                dve(lambda e, bk=bk, s=s: e.tensor_tensor(out=rtf_[s], in0=bank(bk), in1=rtf_[s], op=ALU.add), [PB[bk], brtf[s]], [brtf[s]])
                P.op("sp", lambda e, s=s, rows=rows, cg=cg, dst_d=dst_d: e.dma_start(out=dst_d[rows, cg * 512:(cg + 1) * 512], in_=rtf_[s]), reads=[brtf[s]], writes=[brun[t][cg]],
                     dma_key="ow%d" % s, final=(u == NU - 1))
                if t == NT - 1:
                    DP.release(u * 4 + cg)
            dcnt[0] += len(steps)
        P.emit()
    return nc


def _blkA(W, cols):
    sub = W[:, cols]
    n = sub.shape[1]
    return np.ascontiguousarray(sub.reshape(16, 128, n).transpose(1, 0, 2).reshape(128, 16 * n))


def _prep_weights(w_in, w_mem_kv, w_o, w_gate, w_up, w_down):
    r = np.arange
    blocks = []
    for j in range(4):
        blocks.append(_blkA(w_in, np.concatenate([r(j * 128, (j + 1) * 128), 512 + r(j * 128, (j + 1) * 128)])))
    for jj in range(2):
        blocks.append(_blkA(w_in, 1024 + r(jj * 256, (jj + 1) * 256)))
    for hp in range(4):
        for base in (1536, 2560, 3584):
            blocks.append(_blkA(w_in, base + r(hp * 256, (hp + 1) * 256)))
    for jj in range(2):
        blocks.append(_blkA(w_in, 4608 + r(jj * 256, (jj + 1) * 256)))
    for jj in range(4):
        blocks.append(_blkA(w_mem_kv, r(jj * 256, (jj + 1) * 256)))
    for half in range(2):
        for cg in range(4):
            sub = w_o[half * 1024:(half + 1) * 1024, cg * 512:(cg + 1) * 512]
            blocks.append(np.ascontiguousarray(sub.reshape(8, 128, 512).transpose(1, 0, 2).reshape(128, 4096)))
    wmix = np.stack(blocks, 0).astype(np.float32)
    assert wmix.shape == (32, 128, 4096)
    wgu = np.empty((88, 128, 2048), np.float32)
    for hc in range(NHC):
        wgu[2 * hc] = _blkA(w_gate, r(hc * 128, (hc + 1) * 128))
        wgu[2 * hc + 1] = _blkA(w_up, r(hc * 128, (hc + 1) * 128))
    wd = np.empty((16, 128, UC * 512), np.float32)
    for u in range(NU):
        for cg in range(4):
            sub = w_down[u * UC * 128:(u + 1) * UC * 128, cg * 512:(cg + 1) * 512]
            wd[u * 4 + cg] = sub.reshape(UC, 128, 512).transpose(1, 0, 2).reshape(128, UC * 512)
    return wmix, wgu, wd


def _prep_consts(g_mix, g_mem, conv_w, g_conv_out, g_dq, g_dk, lam_q1, lam_k1, lam_q2, lam_k2, g_sub, g_mq, g_mk, g_mem_out, g_ffn):
    c = np.zeros((128, NCST), np.float32)
    c[:, 0:128] = np.eye(128, dtype=np.float32)
    kk = np.arange(128)
    c[:, 128:256] = (kk[:, None] <= kk[None, :]).astype(np.float32)
    c[:, 256:272] = g_mix[0].reshape(16, 128).T
    c[:, 272:288] = g_mem[0].reshape(16, 128).T
    c[:, 288:304] = g_ffn[0].reshape(16, 128).T
    c[:, 304:316] = conv_w[0].reshape(3, 4, 128).transpose(2, 1, 0).reshape(128, 12)
    c[:, 316:320] = g_conv_out[0].reshape(4, 128).T
    c[:, 320] = g_mq[0]
    c[:, 321] = g_mk[0]
    c[:, 322:834] = np.concatenate([np.tile(g_dq[0], 4), np.tile(g_dk[0], 4)])[None, :]
    c[:, 834:962] = g_sub[0][None, :]
    c[:, 962:1474] = g_mem_out[0][None, :]
    c[:, 1474:1730] = np.concatenate([lam_q1[0], lam_k1[0], lam_q2[0], lam_k2[0]])[None, :]
    c[:, 1730] = g_sub[0]
    return c


_NC_CACHE = {}


def kernel(x, mem, positions, g_mix, g_mem, w_in, conv_w, g_conv_out, g_dq, g_dk,
           lam_q1, lam_k1, lam_q2, lam_k2, g_sub, w_mem_kv, g_mq, g_mk, g_mem_out,
           w_o, g_ffn, w_gate, w_up, w_down):
    f = lambda a: np.asarray(a, dtype=np.float32)
    x = f(x)
    mem = f(mem)
    positions = np.asarray(positions, dtype=np.int32)
    wmix, wgu, wd = _prep_weights(f(w_in)[0], f(w_mem_kv)[0], f(w_o)[0], f(w_gate)[0], f(w_up)[0], f(w_down)[0])
    cst = _prep_consts(f(g_mix), f(g_mem), f(conv_w), f(g_conv_out), f(g_dq), f(g_dk), f(lam_q1), f(lam_k1), f(lam_q2), f(lam_k2),
                       f(g_sub), f(g_mq), f(g_mk), f(g_mem_out), f(g_ffn))
    if "nc" not in _NC_CACHE:
        _NC_CACHE["nc"] = build()
    nc = _NC_CACHE["nc"]
    in_maps = []
    for b in range(8):
        in_maps.append({
            "x": np.ascontiguousarray(x[b]),
            "mem": np.ascontiguousarray(mem[b]),
            "pos": np.ascontiguousarray(positions[b].reshape(16, 128).T),
            "cst": cst, "wmix": wmix, "wgu": wgu, "wd": wd,
        })
    res = run_bass_kernel_spmd(nc, in_maps, core_ids=list(range(8)))
    return np.stack([r["out"] for r in res.results], 0).astype(np.float32)
```

```python
import contextlib
import numpy as np
import concourse.bass as bass
import concourse.mybir as mybir
from concourse.bass_utils import run_bass_kernel_spmd

F32 = mybir.dt.float32
BF16 = mybir.dt.bfloat16
I32 = mybir.dt.int32
AF = mybir.ActivationFunctionType
ALU = mybir.AluOpType
AX = mybir.AxisListType

ENGS = ("pe", "act", "dve", "pool", "sp")
S = 2048
D = 2048
NT = 16
KC = 16
HID = 5632
NHC = 44
NU = 4
UC = 11
EPS = 1e-6
LAM_INIT = 0.2
NCST = 1731
USE_QK512 = True


class Buf:
    __slots__ = ("name", "w", "r")

    def __init__(self, name):
        self.name = name
        self.w = None
        self.r = {}


class Op:
    __slots__ = ("eng", "fn", "waits", "signal", "tick", "dma", "pos")

    def __init__(self, eng, fn):
        self.eng = eng
        self.fn = fn
        self.waits = []
        self.signal = False
        self.tick = None
        self.dma = None
        self.pos = None


class Prog:
    def __init__(self, nc):
        self.nc = nc
        self.streams = {e: [] for e in ENGS}
        self.waited = {e: {} for e in ENGS}
        self.dma_count = {}
        self.final_tokens = []

    def _add_wait(self, o, eng, t):
        wd = self.waited[eng]
        if t[0] == "c":
            if t[1] == eng and eng == "pe":
                return
            if wd.get(t[1], -1) >= t[2]:
                return
            wd[t[1]] = t[2]
            self.streams[t[1]][t[2]].signal = True
            o.waits.append(t)
        else:
            if wd.get(("d", t[1]), -1) >= t[2]:
                return
            wd[("d", t[1])] = t[2]
            o.waits.append(t)

    def op(self, eng, fn, reads=(), writes=(), dma_key=None, final=False):
        o = Op(eng, fn)
        st = self.streams[eng]
        o.pos = len(st)
        if dma_key is not None:
            v = self.dma_count.get(dma_key, 0) + 16
            self.dma_count[dma_key] = v
            o.dma = (dma_key, v)
            tok = ("d", dma_key, v)
        else:
            tok = ("c", eng, o.pos)
        for b in reads:
            if b.w is not None:
                self._add_wait(o, eng, b.w)
        for b in writes:
            if b.w is not None:
                self._add_wait(o, eng, b.w)
            for t in b.r.values():
                self._add_wait(o, eng, t)
        for b in reads:
            if tok[0] == "c":
                b.r[eng] = tok
            else:
                b.r[("d", dma_key)] = tok
        for b in writes:
            b.w = tok
            b.r = {}
        st.append(o)
        if final:
            self.final_tokens.append(tok)
        return tok

    def barrier(self):
        last = {}
        for e in ENGS:
            st = self.streams[e]
            for i in range(len(st) - 1, -1, -1):
                if st[i].fn is not None and st[i].dma is None:
                    last[e] = ("c", e, i)
                    break
        dtoks = [("d", k, v) for k, v in self.dma_count.items()]
        for e in ENGS:
            o = Op(e, None)
            o.pos = len(self.streams[e])
            for f, t in last.items():
                if f != e:
                    self._add_wait(o, e, t)
            for t in dtoks:
                self._add_wait(o, e, t)
            self.streams[e].append(o)

    def emit(self):
        nc = self.nc
        fin_waits = list(self.final_tokens)
        for t in fin_waits:
            if t[0] == "c":
                self.streams[t[1]][t[2]].signal = True
        for e in ENGS:
            c = 0
            for o in self.streams[e]:
                if o.signal:
                    c += 1
                    o.tick = c
        with contextlib.ExitStack() as es:
            esem = {e: es.enter_context(nc.semaphore("s_" + e)) for e in ENGS}
            dsem = {k: es.enter_context(nc.semaphore("d_%s" % (k,))) for k in self.dma_count}
            block = es.enter_context(nc.Block())
            streams = self.streams

            def run(ename, eng):
                for o in streams[ename]:
                    for t in o.waits:
                        if t[0] == "c":
                            eng.wait_ge(esem[t[1]], streams[t[1]][t[2]].tick)
                        else:
                            eng.wait_ge(dsem[t[1]], t[2])
                    if o.fn is None:
                        continue
                    ins = o.fn(eng)
                    if o.dma is not None:
                        ins.then_inc(dsem[o.dma[0]], 16)
                    elif o.signal:
                        ins.then_inc(esem[ename], 1)
                if ename == "sp":
                    for t in fin_waits:
                        if t[0] == "d":
                            eng.wait_ge(dsem[t[1]], t[2])
                        else:
                            eng.wait_ge(esem[t[1]], streams[t[1]][t[2]].tick)

            @block.tensor
            def _(eng):
                run("pe", eng)

            @block.scalar
            def _(eng):
                run("act", eng)

            @block.vector
            def _(eng):
                run("dve", eng)

            @block.gpsimd
            def _(eng):
                run("pool", eng)

            @block.sync
            def _(eng):
                run("sp", eng)


class Arena:
    def __init__(self, ap, n):
        self.ap, self.n, self.off = ap, n, 0

    def alloc(self, nelem, dtype=BF16):
        mult = 2 if dtype in (F32, I32) else 1
        self.off = (self.off + 15) // 16 * 16
        need = nelem * mult
        v = self.ap[:, self.off:self.off + need]
        self.off += need
        assert self.off <= self.n, ("SBUF arena overflow", self.off, self.n)
        return v.bitcast(dtype) if dtype != BF16 else v


class SlotPool:
    def __init__(self, P, name, slots, loads):
        self.P = P
        self.name = name
        self.slots = slots
        self.bufs = [Buf("%s%d" % (name, i)) for i in range(len(slots))]
        self.loads = loads
        self.next = 0
        self.free = list(range(len(slots)))
        self.where = {}
        self.pump()

    def pump(self):
        while self.free and self.next < len(self.loads):
            si = self.free.pop(0)
            key, fn = self.loads[self.next]
            self.next += 1
            o_ap, i_ap = fn(self.slots[si])
            self.P.op("pool", lambda e, o_ap=o_ap, i_ap=i_ap: e.dma_start(out=o_ap, in_=i_ap),
                      writes=[self.bufs[si]], dma_key="%s%d" % (self.name, si))
            self.where[key] = si

    def get(self, key):
        si = self.where[key]
        return self.slots[si], self.bufs[si]

    def release(self, key):
        si = self.where.pop(key)
        self.free.append(si)
        self.pump()


def build(debug=False):
    nc = bass.Bass("TRN2", target_bir_lowering=False)
    dt = nc.dram_tensor
    x_d = dt("x", [S, D], F32, kind="ExternalInput").ap()
    mem_d = dt("mem", [256, D], F32, kind="ExternalInput").ap()
    pos_d = dt("pos", [128, 16], I32, kind="ExternalInput").ap()
    cst_d = dt("cst", [128, NCST], F32, kind="ExternalInput").ap()
    wmix_d = dt("wmix", [32, 128, 4096], F32, kind="ExternalInput").ap()
    wgu_d = dt("wgu", [88, 128, 2048], F32, kind="ExternalInput").ap()
    wd_d = dt("wd", [16, 128, 5632], F32, kind="ExternalInput").ap()
    out_d = dt("out", [S, D], F32, kind="ExternalOutput").ap()
    run_d = dt("run", [S, D], F32, kind="Internal").ap()
    if debug:
        dbg_hT = dt("dbg_hT", [128, KC * S], BF16, kind="ExternalOutput").ap()
        dbg_mixA = dt("dbg_mixA", [128, 8 * S], BF16, kind="ExternalOutput").ap()
        dbg_mixB = dt("dbg_mixB", [128, 8 * S], BF16, kind="ExternalOutput").ap()
        dbg_x1 = dt("dbg_x1", [S, D], F32, kind="ExternalOutput").ap()
        dbg_hfT = dt("dbg_hfT", [128, KC * S], BF16, kind="ExternalOutput").ap()
        dbg_qkt = dt("dbg_qkt", [128, 4 * S], BF16, kind="ExternalOutput").ap()
        dbg_vx = dt("dbg_vx", [128, NT * 2 * 129], BF16, kind="ExternalOutput").ap()
        dbg_kmt = dt("dbg_kmt", [128, 1024], BF16, kind="ExternalOutput").ap()
        dbg_kraw = dt("dbg_kraw", [128, 256], F32, kind="ExternalOutput").ap()
        dbg_ssq = dt("dbg_ssq", [128, 300], F32, kind="ExternalOutput").ap()
        dbg_rim = dt("dbg_rim", [128, 256], F32, kind="ExternalOutput").ap()
        dbg_memT = dt("dbg_memT", [128, KC * 256], BF16, kind="ExternalOutput").ap()
        dbg_vm = dt("dbg_vm", [128, 1032], BF16, kind="ExternalOutput").ap()
        dbg_qmt = dt("dbg_qmt", [128, 4 * S], BF16, kind="ExternalOutput").ap()

    with contextlib.ExitStack() as es:
        NAR = 104512
        arena_t = es.enter_context(nc.sbuf_tensor("arena", [128, NAR], BF16))
        ps = es.enter_context(nc.psum_tensor("ps", [128, 4096], F32))
        AR = Arena(arena_t[:, :], NAR)
        P = Prog(nc)

        def bank(i, n=512, o=0):
            return ps[:, 512 * i + o:512 * i + o + n]

        def bankb(i):
            return ps[:, 512 * i:512 * i + 512].bitcast(BF16)

        PB = [Buf("ps%d" % i) for i in range(8)]

        def dve(fn, reads=(), writes=()):
            return P.op("dve", fn, reads, writes)

        def act(fn, reads=(), writes=()):
            return P.op("act", fn, reads, writes)

        def pe(fn, reads=(), writes=()):
            return P.op("pe", fn, reads, writes)

        cst = AR.alloc(NCST, F32)
        bcst = Buf("cst")
        P.op("sp", lambda e: e.dma_start(out=cst, in_=cst_d), writes=[bcst], dma_key="cst")
        c_ident = cst[:, 0:128]
        c_mask = cst[:, 128:256]
        c_gmix = cst[:, 256:272]
        c_gmem = cst[:, 272:288]
        c_gffn = cst[:, 288:304]
        c_convw = cst[:, 304:316].rearrange("p (j k) -> p j k", j=4)
        c_gconv = cst[:, 316:320]
        c_gmq = cst[:, 320:321]
        c_gmk = cst[:, 321:322]
        c_gqk = cst[:, 322:834]
        c_gsub = cst[:, 834:962]
        c_gmo = cst[:, 962:1474]
        c_lam = cst[:, 1474:1730]
        c_gsubf = cst[:, 1730:1731]
        identb = AR.alloc(128)
        maskb = AR.alloc(128)
        onesb = AR.alloc(128)
        masknb = AR.alloc(128)
        epsb = AR.alloc(1, F32)
        gsub08 = AR.alloc(128, F32)
        gsub08f = AR.alloc(1, F32)
        lamt = AR.alloc(8, F32)
        lamtmp = AR.alloc(64, F32)
        cosT = AR.alloc(128, F32)
        sinT = AR.alloc(128, F32)
        posi = AR.alloc(16, I32)
        posf = AR.alloc(16, F32)
        rtf = AR.alloc(128, F32)
        rti = AR.alloc(128, I32)
        rmk = AR.alloc(128, F32)
        KmT = AR.alloc(4 * 256)
        Vm = AR.alloc(2 * 4 * 129 + 8)
        bsm = Buf("small")
        blam = Buf("lam")
        brope = Buf("rope")
        bKmT = Buf("KmT")
        bVm = Buf("Vm")

        dve(lambda e: e.tensor_copy(identb, c_ident), [bcst], [bsm])
        dve(lambda e: e.tensor_copy(maskb, c_mask), [bcst], [bsm])
        dve(lambda e: e.memset(onesb, 1.0), [], [bsm])
        dve(lambda e: e.tensor_scalar(out=masknb, in0=c_mask, scalar1=-1.0, scalar2=30000.0, op0=ALU.add, op1=ALU.mult), [bcst], [bsm])
        dve(lambda e: e.memset(epsb, EPS), [], [bsm])
        dve(lambda e: e.tensor_scalar(out=gsub08, in0=c_gsub, scalar1=1.0 - LAM_INIT, scalar2=None, op0=ALU.mult), [bcst], [bsm])
        dve(lambda e: e.memset(Vm, 1.0), [], [bVm])
        dve(lambda e: e.tensor_scalar(out=gsub08f, in0=c_gsubf, scalar1=1.0 - LAM_INIT, scalar2=None, op0=ALU.mult), [bcst], [bsm])
        for i in range(2):
            dve(lambda e, i=i: e.tensor_tensor(out=lamtmp, in0=c_lam[:, 128 * i:128 * i + 64], in1=c_lam[:, 128 * i + 64:128 * i + 128], op=ALU.mult), [bcst, blam], [blam])
            dve(lambda e, i=i: e.reduce_sum(out=lamt[:, i:i + 1], in_=lamtmp, axis=AX.X), [blam], [blam])
        act(lambda e: e.activation(out=lamt[:, 0:2], in_=lamt[:, 0:2], func=AF.Exp), [blam], [blam])
        dve(lambda e: e.tensor_tensor(out=lamt[:, 2:3], in0=lamt[:, 1:2], in1=lamt[:, 0:1], op=ALU.subtract), [blam], [blam])
        dve(lambda e: e.tensor_scalar(out=lamt[:, 2:3], in0=lamt[:, 2:3], scalar1=-LAM_INIT, scalar2=None, op0=ALU.add), [blam], [blam])
        neglam = lamt[:, 2:3]
        P.op("sp", lambda e: e.dma_start(out=posi, in_=pos_d), writes=[brope], dma_key="pos")
        dve(lambda e: e.tensor_copy(posf, posi), [brope], [brope])
        TWO_PI = float(2 * np.pi)
        for tab, shift in ((sinT, 0.0), (cosT, float(np.pi / 2))):
            t3 = tab.rearrange("p (t f) -> p t f", t=16)
            for i in range(8):
                fr = float(np.float32(500000.0) ** (-np.float32(2 * i) / np.float32(16)))
                dve(lambda e, i=i, fr=fr, t3=t3: e.tensor_scalar(out=t3[:, :, i], in0=posf, scalar1=fr, scalar2=None, op0=ALU.mult), [brope], [brope])
            if shift:
                dve(lambda e, tab=tab, shift=shift: e.tensor_scalar(out=tab, in0=tab, scalar1=shift, scalar2=None, op0=ALU.add), [brope], [brope])
            dve(lambda e, tab=tab: e.tensor_scalar(out=rtf, in0=tab, scalar1=1.0 / TWO_PI, scalar2=None, op0=ALU.mult), [brope], [brope])
            dve(lambda e: e.tensor_copy(rti, rtf), [brope], [brope])
            dve(lambda e: e.tensor_copy(rtf, rti), [brope], [brope])
            dve(lambda e, tab=tab: e.scalar_tensor_tensor(out=tab, in0=rtf, scalar=-TWO_PI, in1=tab, op0=ALU.mult, op1=ALU.add), [brope], [brope])
            dve(lambda e, tab=tab: e.tensor_scalar(out=rmk, in0=tab, scalar1=float(np.pi), scalar2=None, op0=ALU.is_gt), [brope], [brope])
            dve(lambda e, tab=tab: e.scalar_tensor_tensor(out=tab, in0=rmk, scalar=-TWO_PI, in1=tab, op0=ALU.mult, op1=ALU.add), [brope], [brope])
            dve(lambda e, tab=tab: e.tensor_scalar(out=rmk, in0=tab, scalar1=float(-np.pi), scalar2=None, op0=ALU.is_lt), [brope], [brope])
            dve(lambda e, tab=tab: e.scalar_tensor_tensor(out=tab, in0=rmk, scalar=TWO_PI, in1=tab, op0=ALU.mult, op1=ALU.add), [brope], [brope])
            dve(lambda e, tab=tab: e.tensor_scalar(out=tab, in0=tab, scalar1=3.1415925, scalar2=-3.1415925, op0=ALU.min, op1=ALU.max), [brope], [brope])
            act(lambda e, tab=tab: e.activation(out=tab, in_=tab, func=AF.Sin), [brope], [brope])
        cos3 = cosT.rearrange("p (t f) -> p t f", t=16)
        sin3 = sinT.rearrange("p (t f) -> p t f", t=16)

        hT = AR.alloc(KC * S).rearrange("p (k t) -> p k t", k=KC)
        bhT = [Buf("hT%d" % t) for t in range(NT)]
        mark_ffn = AR.off
        mixT = AR.alloc(8 * S).rearrange("p (c t) -> p c t", c=8)
        bmix = [Buf("mix%d" % c) for c in range(8)]
        QKT = AR.alloc(4 * S).rearrange("p (j t) -> p j t", j=4)
        bQKT = Buf("QKT")
        Vx = AR.alloc(NT * 2 * 129 + 16)
        Vx4 = Vx[:, 0:NT * 2 * 129].rearrange("p (t h d) -> p t h d", t=NT, h=2)
        bVx = Buf("Vx")
        dve(lambda e: e.memset(Vx, 1.0), [], [bVx])
        NSLOT = 6
        slots_all = AR.alloc(4096 * NSLOT)
        wslots = [slots_all[:, i * 4096:(i + 1) * 4096] for i in range(NSLOT)]
        mark_scr = AR.off

        def wl(blk, k, n):
            def fn(slot):
                return (slot.rearrange("p (k n) -> p k n", k=k), wmix_d[blk].rearrange("p (k n) -> p k n", k=k))
            return (blk, fn)

        order = [0, 1, 2, 3, 4, 5, 6, 7, 8, 9, 10, 11, 24, 25, 26, 27, 12, 13, 14, 15, 16, 17, 18, 19, 20, 21, 22, 23, 28, 29, 30, 31]
        loads = [wl(b, 8 if b >= 24 else 16, 512 if b >= 24 else 256) for b in order]
        WP = SlotPool(P, "w", wslots, loads)

        def wv16(slot):
            return slot.rearrange("p (k n) -> p k n", k=16)

        def rsqrt_act(dst, src, n, reads, writes):
            act(lambda e: e.activation(out=dst, in_=src, func=AF.Ln, scale=1.0 / n, bias=epsb), list(reads) + [bsm], writes)
            act(lambda e: e.activation(out=dst, in_=dst, func=AF.Exp, scale=-0.5), writes, writes)

        trb = [0]

        def norm_transpose(src, bsrc, xb, bxb, st1, bst, g_fm, dstT, dcols, bdst):
            act(lambda e: e.activation(out=xb, in_=src, func=AF.Square, accum_out=st1[:, 0:1]), [bsrc], [bxb, bst])
            rsqrt_act(st1[:, 1:2], st1[:, 0:1], float(D), [bst], [bst])
            act(lambda e: e.activation(out=xb, in_=src, func=AF.Copy, scale=st1[:, 1:2]), [bsrc, bst], [bxb])
            for g in range(2):
                bi = 6 + (trb[0] % 2)
                trb[0] += 1
                pv = bankb(bi)

                def tr(e, g=g, pv=pv):
                    ins = None
                    for j in range(8):
                        kc = 8 * g + j
                        ins = e.transpose(pv[:, j * 128:(j + 1) * 128], xb[:, kc * 128:(kc + 1) * 128], identb)
                    return ins
                pe(tr, [bxb, bsm], [PB[bi]])
                dve(lambda e, g=g, pv=pv: e.tensor_tensor(out=dstT[:, 8 * g:8 * g + 8, dcols], in0=pv.rearrange("p (k t) -> p k t", k=8),
                                                          in1=g_fm[:, 8 * g:8 * g + 8].unsqueeze(2).to_broadcast([128, 8, 128]), op=ALU.mult),
                    [PB[bi], bcst], [bdst])

        xs = [AR.alloc(D, F32) for _ in range(2)]
        bxs = [Buf("xs0"), Buf("xs1")]
        xb = AR.alloc(D)
        bxb = Buf("xb")
        st1 = AR.alloc(8, F32)
        bst = Buf("st1")
        for t in range(NT):
            s = t % 2
            P.op("sp", lambda e, t=t, s=s: e.dma_start(out=xs[s], in_=x_d[t * 128:(t + 1) * 128, :]), writes=[bxs[s]], dma_key="xs%d" % s)
            norm_transpose(xs[s], bxs[s], xb, bxb, st1, bst, c_gmix, hT, slice(t * 128, (t + 1) * 128), bhT[t])

        P.barrier()
        if debug:
            P.op("sp", lambda e: e.dma_start(out=dbg_hT, in_=hT.rearrange("p k t -> p (k t)")), reads=bhT, dma_key="dbg0", final=True)
        AR.off = mark_scr
        z = AR.alloc(2064, F32)
        bz = Buf("z")
        uS = AR.alloc(512, F32)
        y1 = AR.alloc(512, F32)
        y2 = AR.alloc(512, F32)
        sqbs = [AR.alloc(512), AR.alloc(512)]
        bsqs = [Buf("sqb0"), Buf("sqb1")]
        cdef = []
        rinvc = AR.alloc(512, F32)
        bcs = Buf("convscr")
        zh = AR.alloc(8, F32).rearrange("p (j k) -> p j k", j=4)
        brc = Buf("rinvc")

        def conv_norm(tg_):
            tsl_ = slice(tg_ * 512, (tg_ + 1) * 512)
            sb_k = 6 + (tg_ % 2)
            rsqrt_act(rinvc, bank(sb_k), 512.0, [PB[sb_k]], [brc])
            for j_ in range(4):
                dve(lambda e, j_=j_: e.scalar_tensor_tensor(out=mixT[:, j_, tsl_], in0=mixT[:, j_, tsl_], scalar=c_gconv[:, j_:j_ + 1], in1=rinvc, op0=ALU.mult, op1=ALU.mult),
                    [bmix[j_], bcst, brc], [bmix[j_]])
        for tg in range(4):
            tsl = slice(tg * 512, (tg + 1) * 512)
            ssq_bank = 6 + (tg % 2)
            for j in range(4):
                set_ = (tg * 4 + j) % 2
                b_u, b_c, b_b = 3 * set_, 3 * set_ + 1, 3 * set_ + 2
                sA, bA = WP.get(j)
                sB, bB = WP.get(4 + j // 2)
                wA = wv16(sA)
                wB = wv16(sB)

                def mm(e, wA=wA, wB=wB, j=j, b_u=b_u, b_c=b_c, b_b=b_b, tsl=tsl):
                    ins = None
                    for (bk, w, c0) in ((b_u, wA, 0), (b_c, wA, 128), (b_b, wB, 128 * (j % 2))):
                        for kc in range(KC):
                            ins = e.matmul(bank(bk), lhsT=w[:, kc, c0:c0 + 128], rhs=hT[:, kc, tsl], start=(kc == 0), stop=(kc == KC - 1))
                    return ins
                pe(mm, [bA, bB] + bhT[4 * tg:4 * tg + 4], [PB[b_u], PB[b_c], PB[b_b]])
                while cdef:
                    cdef.pop(0)()
                if j == 0 and tg > 0:
                    conv_norm(tg - 1)
                zc = z[:, 16:16 + 512]
                if tg == 0:
                    dve(lambda e: e.memset(z[:, 0:16], 0.0), [bz], [bz])
                else:
                    dve(lambda e, j=j: e.tensor_copy(z[:, 14:16], zh[:, j, :]), [bz], [bz])
                act(lambda e, b_u=b_u: e.activation(out=uS, in_=bank(b_u), func=AF.Copy), [PB[b_u]], [bcs])
                dve(lambda e, b_c=b_c: e.tensor_tensor(out=zc, in0=bank(b_c), in1=uS, op=ALU.mult), [PB[b_c], bcs], [bz])
                dve(lambda e, j=j: e.tensor_scalar(out=y1, in0=zc, scalar1=c_convw[:, j, 2:3], scalar2=None, op0=ALU.mult), [bz, bcst], [bcs])
                dve(lambda e, j=j: e.scalar_tensor_tensor(out=y2, in0=z[:, 15:15 + 512], scalar=c_convw[:, j, 1:2], in1=y1, op0=ALU.mult, op1=ALU.add), [bz, bcst, bcs], [bcs])
                dve(lambda e, j=j: e.scalar_tensor_tensor(out=y1, in0=z[:, 14:14 + 512], scalar=c_convw[:, j, 0:1], in1=y2, op0=ALU.mult, op1=ALU.add), [bz, bcst, bcs], [bcs])
                dve(lambda e, b_b=b_b: e.tensor_tensor(out=y2, in0=bank(b_b), in1=y1, op=ALU.mult), [PB[b_b], bcs], [bcs])
                dve(lambda e, j=j: e.tensor_copy(zh[:, j, :], z[:, 16 + 510:16 + 512]), [bz], [bz])
                act(lambda e, j=j, tsl=tsl: e.activation(out=mixT[:, j, tsl], in_=y2, func=AF.Copy), [bcs], [bmix[j]])
                sqb, bsq = sqbs[(tg * 4 + j) % 2], bsqs[(tg * 4 + j) % 2]
                act(lambda e, sqb=sqb: e.activation(out=sqb, in_=y2, func=AF.Square), [bcs], [bsq])
                cdef.append(lambda j=j, ssq_bank=ssq_bank, sqb=sqb, bsq=bsq: pe(lambda e: e.matmul(bank(ssq_bank), lhsT=onesb, rhs=sqb, start=(j == 0), stop=(j == 3)), [bsq, bsm], [PB[ssq_bank]]))
        while cdef:
            cdef.pop(0)()
        conv_norm(3)
        for k in range(6):
            WP.release(k)

        def head_pair(hp):
            P.barrier()
            AR.off = mark_scr
            NSET = 2
            sqs = [AR.alloc(512) for _ in range(NSET)]
            qns = [AR.alloc(512, F32) for _ in range(NSET)]
            qbs = [AR.alloc(512) for _ in range(NSET)]
            st8s = [AR.alloc(16, F32) for _ in range(NSET)]
            rts = [[AR.alloc(64, F32) for _ in range(4)] for _ in range(NSET)]
            bqs = [Buf("qscr%d" % i) for i in range(NSET)]
            blk0 = 6 + 3 * hp
            sQ, bWQ = WP.get(blk0)
            sK, bWK = WP.get(blk0 + 1)
            sV, bWV = WP.get(blk0 + 2)
            wQ, wK, wV = wv16(sQ), wv16(sK), wv16(sV)
            iq, ik = WP.where[blk0], WP.where[blk0 + 1]
            wQK = None
            if USE_QK512 and ik == iq + 1:
                wQK = slots_all[:, iq * 4096:(iq + 2) * 4096].rearrange("p (s k n) -> p s k n", s=2, k=16)
            deferred = []
            for t in range(NT):
                tsl = slice(t * 128, (t + 1) * 128)
                bqk = 2 * (t % 2)
                bv = bqk + 1
                btr = 4 + (t % 2)
                si = t % NSET
                sq, qn, qb, st8, rt, bq = sqs[si], qns[si], qbs[si], st8s[si], rts[si], bqs[si]

                def mm(e, tsl=tsl, bqk=bqk, bv=bv):
                    ins = None
                    if wQK is not None:
                        for kc in range(KC):
                            ins = e.matmul(bank(bqk), lhsT=hT[:, kc, tsl], rhs=wQK[:, :, kc, :], start=(kc == 0), stop=(kc == KC - 1))
                        grp = ((wV, bv, 0),)
                    else:
                        grp = ((wQ, bqk, 0), (wK, bqk, 256), (wV, bv, 0))
                    for (w, bk, c0) in grp:
                        for kc in range(KC):
                            ins = e.matmul(bank(bk, 256, c0), lhsT=hT[:, kc, tsl], rhs=w[:, kc, :], start=(kc == 0), stop=(kc == KC - 1))
                    return ins
                pe(mm, [bWQ, bWK, bWV, bhT[t]], [PB[bqk], PB[bv]])
                while deferred:
                    deferred.pop(0)()
                act(lambda e, bv=bv, t=t: e.activation(out=Vx4[:, t, :, 0:128], in_=bank(bv, 256).rearrange("p (h d) -> p h d", h=2), func=AF.Copy), [PB[bv]], [bVx])
                act(lambda e, bqk=bqk, sq=sq: e.activation(out=sq, in_=bank(bqk), func=AF.Square), [PB[bqk]], [bq])
                dve(lambda e, sq=sq, st8=st8: e.reduce_sum(out=st8[:, 0:8], in_=sq.rearrange("p (g d) -> p g d", g=8), axis=AX.X), [bq], [bq])
                rsqrt_act(st8[:, 8:16], st8[:, 0:8], 64.0, [bq], [bq])
                dve(lambda e, bqk=bqk, qn=qn, st8=st8: e.tensor_tensor(out=qn.rearrange("p (g d) -> p g d", g=8), in0=bank(bqk).rearrange("p (g d) -> p g d", g=8),
                                                                       in1=st8[:, 8:16].unsqueeze(2).to_broadcast([128, 8, 64]), op=ALU.mult), [PB[bqk], bq], [bq])
                dve(lambda e, qn=qn: e.tensor_tensor(out=qn, in0=qn, in1=c_gqk, op=ALU.mult), [bq, bcst], [bq])
                brr = [Buf("rr%d" % k_) for k_ in range(4)]
                bqb = Buf("qb")
                dve(lambda e, qn=qn, qb=qb: e.tensor_copy(qb, qn), [bq], [bqb])
                qg3 = qn.rearrange("p (g d) -> p g d", g=8)
                qb3 = qb.rearrange("p (g d) -> p g d", g=8)
                r1_, r2_ = qg3[:, :, 0:8], qg3[:, :, 8:16]
                cb = cos3[:, t, :].unsqueeze(1).to_broadcast([128, 8, 8])
                sb_ = sin3[:, t, :].unsqueeze(1).to_broadcast([128, 8, 8])
                t4 = [r.rearrange("p (g d) -> p g d", g=8) for r in rt]
                dve(lambda e, cb=cb, t4=t4, r1_=r1_: e.tensor_tensor(out=t4[0], in0=r1_, in1=cb, op=ALU.mult), [bq, brope], [brr[0]])
                dve(lambda e, sb_=sb_, t4=t4, r2_=r2_: e.tensor_tensor(out=t4[1], in0=r2_, in1=sb_, op=ALU.mult), [bq, brope], [brr[1]])
                dve(lambda e, cb=cb, t4=t4, r2_=r2_: e.tensor_tensor(out=t4[2], in0=r2_, in1=cb, op=ALU.mult), [bq, brope], [brr[2]])
                dve(lambda e, sb_=sb_, t4=t4, r1_=r1_: e.tensor_tensor(out=t4[3], in0=r1_, in1=sb_, op=ALU.mult), [bq, brope], [brr[3]])
                dve(lambda e, t4=t4, qb3=qb3: e.tensor_tensor(out=qb3[:, :, 0:8], in0=t4[0], in1=t4[1], op=ALU.subtract), [brr[0], brr[1], bqb], [bqb])
                dve(lambda e, t4=t4, qb3=qb3: e.tensor_tensor(out=qb3[:, :, 8:16], in0=t4[2], in1=t4[3], op=ALU.add), [brr[2], brr[3], bqb], [bqb])
                pvb = bankb(btr)

                def tr(e, pvb=pvb, qb=qb):
                    ins = None
                    for jj in range(4):
                        ins = e.transpose(pvb[:, jj * 128:(jj + 1) * 128], qb[:, jj * 128:(jj + 1) * 128], identb)
                    return ins
                def fin(tr=tr, bq=bqb, btr=btr, pvb=pvb, tsl=tsl):
                    pe(tr, [bq, bsm], [PB[btr]])
                    act(lambda e: e.activation(out=QKT[:, :, tsl], in_=pvb[:, 0:512].rearrange("p (j t) -> p j t", j=4), func=AF.Copy), [PB[btr]], [bQKT])
                deferred.append(fin)
            while deferred:
                deferred.pop(0)()
            WP.release(blk0)
            WP.release(blk0 + 1)
            WP.release(blk0 + 2)
            P.barrier()
            AR.off = mark_scr
            PT = [AR.alloc(1024) for _ in range(2)]
            bPT = [Buf("PT0"), Buf("PT1")]
            c4 = AR.alloc(512, F32)
            c5 = AR.alloc(512, F32)
            c6 = AR.alloc(512, F32)
            c7 = AR.alloc(512, F32)
            sqo = AR.alloc(512)
            bc = [Buf("c4"), Buf("c5"), Buf("c6"), Buf("c7")]
            bsqo = Buf("sqo")
            c4s = [c4, AR.alloc(512, F32)]
            c6s = [c6, AR.alloc(512, F32)]
            b4s = [bc[0], Buf("c4b")]
            b6s = [bc[2], Buf("c6b")]
            gcnt = [0]
            SC = 0.125
            pending = [None]
            for hh in range(2):
                h = 2 * hp + hh
                mch = (4 + h) % 8
                for g in range(4):
                    nkt = 4 * g + 4
                    qks = []
                    for kt in range(nkt):
                        n0 = max(0, kt - 4 * g)
                        w = (4 - n0) * 128
                        coff = n0 * 128
                        sp_ = 2 * (kt % 2)
                        pt = PT[kt % 2]
                        bpt = bPT[kt % 2]
                        ksl = slice(kt * 128, (kt + 1) * 128)
                        qsl = slice(g * 512 + coff, (g + 1) * 512)

                        def qk(e, sp_=sp_, ksl=ksl, qsl=qsl, w=w, hh=hh, diag=(kt >= 4 * g)):
                            ins = None
                            for m in range(2):
                                ins = e.matmul(bank(sp_ + m, w), lhsT=QKT[64 * m:64 * m + 64, 2 + hh, ksl], rhs=QKT[64 * m:64 * m + 64, hh, qsl], start=True, stop=not diag)
                            if diag:
                                for m in range(2):
                                    ins = e.matmul(bank(sp_ + m, 128), lhsT=identb, rhs=masknb, start=False, stop=True)
                            return ins
                        qks.append((qk, sp_))
                    pe(qks[0][0], [bQKT, bsm], [PB[qks[0][1]], PB[qks[0][1] + 1]])
                    for kt in range(nkt):
                        n0 = max(0, kt - 4 * g)
                        w = (4 - n0) * 128
                        coff = n0 * 128
                        sp_ = 2 * (kt % 2)
                        pt = PT[kt % 2]
                        bpt = bPT[kt % 2]
                        src = ps[:, 512 * sp_:512 * sp_ + 1024].rearrange("p (m c) -> p m c", m=2)[:, :, 0:w]
                        dst = pt.rearrange("p (m c) -> p m c", m=2)[:, :, 0:w]
                        act(lambda e, src=src, dst=dst: e.activation(out=dst, in_=src, func=AF.Exp, scale=SC), [PB[sp_], PB[sp_ + 1]], [bpt])

                        def pvm(m, pt=pt, kt=kt, hh=hh, w=w, coff=coff, first=(kt == 0), last=(kt == nkt - 1)):
                            def f(e):
                                e.matmul(bank(4 + m, w, coff), lhsT=Vx4[:, kt, hh, 0:128], rhs=pt[:, m * 512:m * 512 + w], start=first, stop=last)
                                return e.matmul(bank(6 + m, w, coff), lhsT=onesb, rhs=pt[:, m * 512:m * 512 + w], start=first, stop=last)
                            return f
                        if kt + 1 < nkt:
                            pe(qks[kt + 1][0], [bQKT, bsm], [PB[qks[kt + 1][1]], PB[qks[kt + 1][1] + 1]])
                        pe(pvm(0), [bpt, bVx, bsm], [PB[4], PB[6]])
                        pe(pvm(1), [bpt, bVx, bsm], [PB[5], PB[7]])
                        if kt == min(7, nkt - 1) and pending[0] is not None:
                            pending[0]()
                            pending[0] = None
                    par = gcnt[0] % 2
                    gcnt[0] += 1
                    c4_, c6_, b4_, b6_ = c4s[par], c6s[par], b4s[par], b6s[par]
                    for cb_, bk_, bb_ in ((c4_, 4, b4_), (c6_, 6, b6_), (c5, 5, bc[1]), (c7, 7, bc[3])):
                        dve(lambda e, cb_=cb_, bk_=bk_: e.tensor_copy(cb_, bank(bk_)), [PB[bk_]], [bb_])
                    dve(lambda e, c4_=c4_: e.tensor_tensor(out=c4_, in0=c4_, in1=c7, op=ALU.mult), [b4_, bc[3]], [b4_])
                    dve(lambda e, c6_=c6_: e.tensor_tensor(out=c5, in0=c5, in1=c6_, op=ALU.mult), [bc[1], b6_], [bc[1]])
                    dve(lambda e, c6_=c6_: e.tensor_tensor(out=c6_, in0=c6_, in1=c7, op=ALU.mult), [b6_, bc[3]], [b6_])
                    dve(lambda e, c4_=c4_: e.scalar_tensor_tensor(out=c4_, in0=c5, scalar=neglam, in1=c4_, op0=ALU.mult, op1=ALU.add), [b4_, bc[1], blam], [b4_])
                    dve(lambda e, c6_=c6_: e.reciprocal(c7, c6_), [b6_], [bc[3]])
                    dve(lambda e, c4_=c4_: e.tensor_tensor(out=c4_, in0=c4_, in1=c7, op=ALU.mult), [b4_, bc[3]], [b4_])

                    def ep2(mch=mch, g=g, c4_=c4_, c6_=c6_, b4_=b4_, b6_=b6_):
                        act(lambda e: e.activation(out=sqo, in_=c4_, func=AF.Square), [b4_], [bsqo])
                        pe(lambda e: e.matmul(bank(2), lhsT=onesb, rhs=sqo, start=True, stop=True), [bsqo, bsm], [PB[2]])
                        rsqrt_act(c6_, bank(2), 128.0, [PB[2]], [b6_])
                        dve(lambda e: e.scalar_tensor_tensor(out=mixT[:, mch, g * 512:(g + 1) * 512], in0=c4_, scalar=gsub08f, in1=c6_, op0=ALU.mult, op1=ALU.mult),
                            [b4_, b6_, bsm], [bmix[mch]])
                    pending[0] = ep2
            pending[0]()

        def wo_pass(half, final):
            P.barrier()
            AR.off = mark_scr
            NST = 8
            stg = [AR.alloc(512, F32) for _ in range(NST)]
            bstg = [Buf("wst%d" % i) for i in range(NST)]
            xb2 = AR.alloc(D)
            bxb2 = Buf("xb2")
            st2 = AR.alloc(8, F32)
            bst2 = Buf("st2")
            blk0 = 24 + 4 * half
            ws = [WP.get(blk0 + cg) for cg in range(4)]
            src_d = x_d if half == 0 else run_d
            steps = [(t, cg) for t in range(NT) for cg in range(4)]
            PF = 4

            def load(i):
                t_, cg_ = steps[i]
                s_ = i % NST
                rows_ = slice(t_ * 128, (t_ + 1) * 128)
                P.op("sp", lambda e: e.dma_start(out=stg[s_], in_=src_d[rows_, cg_ * 512:(cg_ + 1) * 512]), reads=([brun[t_][cg_]] if half else []), writes=[bstg[s_]], dma_key="ws%d" % s_)
            for i in range(PF):
                load(i)
            wdef = []
            for i, (t, cg) in enumerate(steps):
                s = i % NST
                rows = slice(t * 128, (t + 1) * 128)
                if i + PF < len(steps):
                    load(i + PF)
                bk = i % 6
                wsl, wb = ws[cg]
                w8 = wsl.rearrange("p (k n) -> p k n", k=8)

                def mm(e, bk=bk, w8=w8, rows=rows):
                    ins = None
                    for c in range(8):
                        ins = e.matmul(bank(bk), lhsT=mixT[:, c, rows], rhs=w8[:, c, :], start=(c == 0), stop=(c == 7))
                    return ins
                pe(mm, [wb] + bmix, [PB[bk]])
                if cg == 2:
                    while wdef:
                        wdef.pop(0)()
                dve(lambda e, bk=bk, s=s: e.tensor_tensor(out=stg[s], in0=bank(bk), in1=stg[s], op=ALU.add), [PB[bk], bstg[s]], [bstg[s]])
                P.op("sp", lambda e, s=s, rows=rows, cg=cg: e.dma_start(out=run_d[rows, cg * 512:(cg + 1) * 512], in_=stg[s]), reads=[bstg[s]], writes=[brun[t][cg]], dma_key="wo%d" % s)
                if final:
                    act(lambda e, s=s, cg=cg: e.activation(out=QKT[:, 0, cg * 512:(cg + 1) * 512], in_=stg[s], func=AF.Square, accum_out=st2[:, cg:cg + 1]), [bstg[s]], [bQKT, bst2])
                    if cg == 3:
                        dve(lambda e: e.reduce_sum(out=st2[:, 4:5], in_=st2[:, 0:4], axis=AX.X), [bst2], [bst2])
                        rsqrt_act(st2[:, 5:6], st2[:, 4:5], float(D), [bst2], [bst2])
                        for c in range(4):
                            sc = (i - 3 + c) % NST
                            act(lambda e, sc=sc, c=c: e.activation(out=xb2[:, c * 512:(c + 1) * 512], in_=stg[sc], func=AF.Copy, scale=st2[:, 5:6]), [bstg[sc], bst2], [bxb2])

                        def fin(t=t, rows=rows):
                            for g in range(2):
                                bi = 6 + (trb[0] % 2)
                                trb[0] += 1
                                pv_ = bankb(bi)

                                def tr(e, g=g, pv_=pv_):
                                    ins = None
                                    for jx in range(8):
                                        kc = 8 * g + jx
                                        ins = e.transpose(pv_[:, jx * 128:(jx + 1) * 128], xb2[:, kc * 128:(kc + 1) * 128], identb)
                                    return ins
                                pe(tr, [bxb2, bsm], [PB[bi]])
                                dve(lambda e, g=g, pv_=pv_: e.tensor_tensor(out=hT[:, 8 * g:8 * g + 8, rows], in0=pv_.rearrange("p (k t) -> p k t", k=8),
                                                                            in1=c_gffn[:, 8 * g:8 * g + 8].unsqueeze(2).to_broadcast([128, 8, 128]), op=ALU.mult),
                                    [PB[bi], bcst], [bhT[t]])
                        wdef.append(fin)
            while wdef:
                wdef.pop(0)()
            for cg in range(4):
                WP.release(blk0 + cg)

        brun = [[Buf("run%d_%d" % (t, c)) for c in range(4)] for t in range(NT)]

        def mem_phase():
            P.barrier()
            AR.off = mark_scr
            ms = AR.alloc(D, F32)
            bms = Buf("ms")
            mb = AR.alloc(D)
            bmb = Buf("mb")
            st3 = AR.alloc(8, F32)
            bst3 = Buf("st3")
            memT = AR.alloc(KC * 256).rearrange("p (k t) -> p k t", k=KC)
            bmemT = Buf("memT")
            sqmK = ms[:, 0:256].bitcast(BF16)
            rimK = ms[:, 512:1024]
            bsc = bms
            for t in range(2):
                P.op("sp", lambda e, t=t: e.dma_start(out=ms, in_=mem_d[t * 128:(t + 1) * 128, :]), writes=[bms], dma_key="ms")
                norm_transpose(ms, bms, mb, bmb, st3, bst3, c_gmem, memT, slice(t * 128, (t + 1) * 128), bmemT)
            for hm in range(4):
                sl, wb = WP.get(20 + hm // 2)
                w = wv16(sl)
                c0 = 128 * (hm % 2)

                def mm(e, w=w, c0=c0):
                    ins = None
                    for kc in range(KC):
                        ins = e.matmul(bank(0, 256), lhsT=w[:, kc, c0:c0 + 128], rhs=memT[:, kc, :], start=(kc == 0), stop=(kc == KC - 1))
                    return ins
                pe(mm, [wb, bmemT], [PB[0]])
                act(lambda e: e.activation(out=sqmK[:, 0:256], in_=bank(0, 256), func=AF.Square), [PB[0]], [bsc])
                pe(lambda e: e.matmul(bank(1, 256), lhsT=onesb, rhs=sqmK[:, 0:256], start=True, stop=True), [bsc, bsm], [PB[1]])
                rsqrt_act(rimK[:, 0:256], bank(1, 256), 128.0, [PB[1]], [bsc])
                dve(lambda e, hm=hm: e.scalar_tensor_tensor(out=KmT[:, hm * 256:(hm + 1) * 256], in0=bank(0, 256), scalar=c_gmk, in1=rimK[:, 0:256], op0=ALU.mult, op1=ALU.mult),
                    [PB[0], bsc, bcst], [bKmT])
            if debug:
                kraw = sqmK.bitcast(F32)
                bkr = Buf("kraw")
                dve(lambda e: e.tensor_copy(kraw, bank(0, 256)), [PB[0]], [bkr])
                P.op("sp", lambda e: e.dma_start(out=dbg_kraw, in_=kraw), reads=[bkr], dma_key="dbga", final=True)
                ssqd = ms[:, 1024:1324]
                dve(lambda e: e.tensor_copy(ssqd[:, 0:256], bank(1, 256)), [PB[1]], [bkr])
                P.op("sp", lambda e: e.dma_start(out=dbg_ssq, in_=ssqd), reads=[bkr], dma_key="dbgd", final=True)
                P.op("sp", lambda e: e.dma_start(out=dbg_rim, in_=rimK[:, 0:256]), reads=[bsc], dma_key="dbgb", final=True)
                P.op("sp", lambda e: e.dma_start(out=dbg_memT, in_=memT.rearrange("p k t -> p (k t)")), reads=[bmemT], dma_key="dbgc", final=True)
            WP.release(20)
            WP.release(21)
            Vm4 = Vm[:, 0:2 * 4 * 129].rearrange("p (t h d) -> p t h d", t=2, h=4)
            for mt in range(2):
                for half in range(2):
                    sl, wb = WP.get(22 + half)
                    w = wv16(sl)

                    def mm(e, w=w, mt=mt):
                        ins = None
                        for kc in range(KC):
                            ins = e.matmul(bank(2, 256), lhsT=memT[:, kc, mt * 128:(mt + 1) * 128], rhs=w[:, kc, :], start=(kc == 0), stop=(kc == KC - 1))
                        return ins
                    pe(mm, [wb, bmemT], [PB[2]])
                    act(lambda e, mt=mt, half=half: e.activation(out=Vm4[:, mt, 2 * half:2 * half + 2, 0:128], in_=bank(2, 256).rearrange("p (h d) -> p h d", h=2), func=AF.Copy), [PB[2]], [bVm])
            WP.release(22)
            WP.release(23)
            P.barrier()
            AR.off = mark_scr
            sqm = AR.alloc(512)
            rim = AR.alloc(512, F32)
            bsc = Buf("mscr2")
            PTm = AR.alloc(1024)
            bPTm = Buf("PTm")
            ym = AR.alloc(4 * 512, F32)
            sqy = AR.alloc(512)
            ybm = AR.alloc(4 * 512)
            st5 = AR.alloc(16, F32)
            bym = Buf("ym")
            QmT = QKT
            qdef = []
            for hm in range(4):
                sl, wb = WP.get(18 + hm // 2)
                w = wv16(sl)
                c0 = 128 * (hm % 2)
                for tg in range(4):
                    tsl = slice(tg * 512, (tg + 1) * 512)
                    bk = (hm * 4 + tg) % 2

                    def mm(e, w=w, c0=c0, tsl=tsl, bk=bk):
                        ins = None
                        for kc in range(KC):
                            ins = e.matmul(bank(bk), lhsT=w[:, kc, c0:c0 + 128], rhs=hT[:, kc, tsl], start=(kc == 0), stop=(kc == KC - 1))
                        return ins
                    pe(mm, [wb] + bhT[4 * tg:4 * tg + 4], [PB[bk]])
                    while qdef:
                        qdef.pop(0)()
                    act(lambda e, bk=bk: e.activation(out=sqm, in_=bank(bk), func=AF.Square), [PB[bk]], [bsc])

                    def qfin(bk=bk, hm=hm, tsl=tsl):
                        pe(lambda e: e.matmul(bank(2 + bk), lhsT=onesb, rhs=sqm, start=True, stop=True), [bsc, bsm], [PB[2 + bk]])
                        rsqrt_act(rim, bank(2 + bk), 128.0, [PB[2 + bk]], [bsc])
                        dve(lambda e: e.scalar_tensor_tensor(out=QmT[:, hm, tsl], in0=bank(bk), scalar=c_gmq, in1=rim, op0=ALU.mult, op1=ALU.mult),
                            [PB[bk], bsc, bcst], [bQKT])
                    qdef.append(qfin)
            while qdef:
                qdef.pop(0)()
            WP.release(18)
            WP.release(19)
            SCM = float(128.0 ** -0.5)
            PTms = [PTm, rim.bitcast(BF16)]
            bPTms = [bPTm, bsc]
            ucnt = [0]

            def m_qk(tg_, hm_, u_):
                tsl_ = slice(tg_ * 512, (tg_ + 1) * 512)
                sb = 2 * (u_ % 2)

                def f(e):
                    ins = None
                    for mt in range(2):
                        ins = e.matmul(bank(sb + mt), lhsT=KmT[:, hm_ * 256 + mt * 128:hm_ * 256 + (mt + 1) * 128], rhs=QmT[:, hm_, tsl_], start=True, stop=True)
                    return ins
                pe(f, [bKmT, bQKT], [PB[sb], PB[sb + 1]])
                ptm = PTms[u_ % 2]
                act(lambda e: e.activation(out=ptm, in_=ps[:, 512 * sb:512 * sb + 1024], func=AF.Exp, scale=SCM), [PB[sb], PB[sb + 1]], [bPTms[u_ % 2]])

            def m_pv(hm_, u_):
                ab = 4 + 2 * (u_ % 2)
                ptm = PTms[u_ % 2]

                def f(e):
                    ins = None
                    for n in range(4):
                        o_ = bank(ab + n // 2, 129, 256 * (n % 2))
                        for mt in range(2):
                            ins = e.matmul(o_, lhsT=ptm[:, mt * 512 + n * 128:mt * 512 + (n + 1) * 128], rhs=Vm4[:, mt, hm_, :], start=(mt == 0), stop=(mt == 1))
                    return ins
                pe(f, [bPTms[u_ % 2], bVm], [PB[ab], PB[ab + 1]])
                acc4 = ps[:, 512 * ab:512 * ab + 1024].rearrange("p (n c) -> p n c", n=4)
                dve(lambda e: e.reciprocal(st5[:, 0:4], acc4[:, :, 128]), [PB[ab], PB[ab + 1], bym], [bym])
                dve(lambda e: e.tensor_tensor(out=ym.rearrange("p (n c) -> p n c", n=4)[:, :, hm_ * 128:(hm_ + 1) * 128], in0=acc4[:, :, 0:128],
                                              in1=st5[:, 0:4].unsqueeze(2).to_broadcast([128, 4, 128]), op=ALU.mult), [PB[ab], PB[ab + 1], bym], [bym])
            for tg in range(4):
                tsl = slice(tg * 512, (tg + 1) * 512)
                u0 = ucnt[0]
                m_qk(tg, 0, u0)
                for hm in range(4):
                    if hm + 1 < 4:
                        m_qk(tg, hm + 1, u0 + hm + 1)
                    m_pv(hm, u0 + hm)
                ucnt[0] += 4
                for n in range(4):
                    yn = ym[:, n * 512:(n + 1) * 512]
                    act(lambda e, yn=yn, n=n: e.activation(out=sqy, in_=yn, func=AF.Square, accum_out=st5[:, 4 + n:5 + n]), [bym], [bym])
                rsqrt_act(st5[:, 8:12], st5[:, 4:8], 512.0, [bym], [bym])
                for n in range(4):
                    yn = ym[:, n * 512:(n + 1) * 512]
                    dve(lambda e, yn=yn, n=n: e.scalar_tensor_tensor(out=ybm[:, n * 512:(n + 1) * 512], in0=yn, scalar=st5[:, 8 + n:9 + n], in1=c_gmo, op0=ALU.mult, op1=ALU.mult), [bym, bcst], [bym])
                for hm in range(4):
                    pvb = bankb(6 + hm % 2)

                    def tr(e, pvb=pvb, hm=hm):
                        ins = None
                        for n in range(4):
                            ins = e.transpose(pvb[:, n * 128:(n + 1) * 128], ybm[:, n * 512 + hm * 128:n * 512 + (hm + 1) * 128], identb)
                        return ins
                    pe(tr, [bym, bsm], [PB[6 + hm % 2]])
                    act(lambda e, pvb=pvb, hm=hm, tsl=tsl: e.activation(out=mixT[:, 4 + hm, tsl], in_=pvb[:, 0:512], func=AF.Copy), [PB[6 + hm % 2]], [bmix[4 + hm]])

        head_pair(0)
        head_pair(1)
        if debug:
            P.barrier()
            P.op("sp", lambda e: e.dma_start(out=dbg_mixA, in_=mixT.rearrange("p c t -> p (c t)")), reads=bmix, dma_key="dbg1", final=True)
            P.op("sp", lambda e: e.dma_start(out=dbg_qkt, in_=QKT.rearrange("p c t -> p (c t)")), reads=[bQKT], dma_key="dbg3", final=True)
            P.op("sp", lambda e: e.dma_start(out=dbg_vx, in_=Vx[:, 0:NT * 2 * 129]), reads=[bVx], dma_key="dbg4", final=True)
        wo_pass(0, False)
        head_pair(2)
        head_pair(3)
        mem_phase()
        if debug:
            P.barrier()
            P.op("sp", lambda e: e.dma_start(out=dbg_mixB, in_=mixT.rearrange("p c t -> p (c t)")), reads=bmix, dma_key="dbg2", final=True)
            P.op("sp", lambda e: e.dma_start(out=dbg_kmt, in_=KmT), reads=[bKmT], dma_key="dbg7", final=True)
            P.op("sp", lambda e: e.dma_start(out=dbg_vm, in_=Vm[:, 0:1032]), reads=[bVm], dma_key="dbg8", final=True)
            P.op("sp", lambda e: e.dma_start(out=dbg_qmt, in_=QKT.rearrange("p c t -> p (c t)")), reads=[bQKT], dma_key="dbg9", final=True)
        wo_pass(1, True)
        if debug:
            P.barrier()
            P.op("sp", lambda e: e.dma_start(out=dbg_x1, in_=run_d), reads=[b_ for bb in brun for b_ in bb], dma_key="dbg5", final=True)
            P.op("sp", lambda e: e.dma_start(out=dbg_hfT, in_=hT.rearrange("p k t -> p (k t)")), reads=bhT, dma_key="dbg6", final=True)

        P.barrier()
        AR.off = mark_ffn
        hfT = hT
        actT = AR.alloc(UC * S).rearrange("p (c t) -> p c t", c=UC)
        bactT = Buf("actT")
        gslots = [AR.alloc(2048) for _ in range(6)]
        dslots = [AR.alloc(UC * 512) for _ in range(2)]
        sg = [AR.alloc(1024) for _ in range(2)]
        bsg = [Buf("sg0"), Buf("sg1")]
        NSTG = 8
        rtf_ = [AR.alloc(512, F32) for _ in range(NSTG)]
        brtf = [Buf("rf%d" % i) for i in range(NSTG)]

        def gl(idx):
            def fn(slot):
                return (slot.rearrange("p (k n) -> p k n", k=16), wgu_d[idx].rearrange("p (k n) -> p k n", k=16))
            return (idx, fn)

        def dl(idx):
            def fn(slot):
                return (slot.rearrange("p (k n) -> p k n", k=UC), wd_d[idx].rearrange("p (k n) -> p k n", k=UC))
            return (idx, fn)
        GP = SlotPool(P, "g", gslots, [gl(i) for i in range(88)])
        DP = SlotPool(P, "dn", dslots, [dl(i) for i in range(16)])
        dcnt = [0]
        for u in range(NU):
            for ci in range(UC):
                hc = u * UC + ci
                sgw, bg = GP.get(2 * hc)
                suw, bu = GP.get(2 * hc + 1)
                wg, wu = wv16(sgw), wv16(suw)
                for tb in range(2):
                    set_ = (ci * 2 + tb) % 2
                    b0 = 4 * set_

                    def mm(e, wg=wg, wu=wu, tb=tb, b0=b0):
                        ins = None
                        for (w, bo_) in ((wg, b0), (wu, b0 + 2)):
                            for kc in range(KC):
                                for hf in range(2):
                                    tsl = slice(tb * 1024 + hf * 512, tb * 1024 + (hf + 1) * 512)
                                    ins = e.matmul(bank(bo_ + hf), lhsT=w[:, kc, :], rhs=hfT[:, kc, tsl], start=(kc == 0), stop=(kc == KC - 1))
                        return ins
                    pe(mm, [bg, bu] + bhT[8 * tb:8 * tb + 8], [PB[b0], PB[b0 + 1], PB[b0 + 2], PB[b0 + 3]])
                    s = set_
                    act(lambda e, b0=b0, s=s: e.activation(out=sg[s], in_=ps[:, 512 * b0:512 * b0 + 1024], func=AF.Silu), [PB[b0], PB[b0 + 1]], [bsg[s]])
                    dve(lambda e, b0=b0, s=s, ci=ci, tb=tb: e.tensor_tensor(out=actT[:, ci, tb * 1024:(tb + 1) * 1024], in0=ps[:, 512 * (b0 + 2):512 * (b0 + 2) + 1024], in1=sg[s], op=ALU.mult),
                        [PB[b0 + 2], PB[b0 + 3], bsg[s]], [bactT])
                GP.release(2 * hc)
                GP.release(2 * hc + 1)
            dst_d = out_d if u == NU - 1 else run_d
            steps = [(cg, t) for cg in range(4) for t in range(NT)]
            PF = 5

            def load(i):
                cg_, t_ = steps[i]
                s_ = (dcnt[0] + i) % NSTG
                rows_ = slice(t_ * 128, (t_ + 1) * 128)
                P.op("sp", lambda e, s_=s_, rows_=rows_, cg_=cg_: e.dma_start(out=rtf_[s_], in_=run_d[rows_, cg_ * 512:(cg_ + 1) * 512]), reads=[brun[t_][cg_]], writes=[brtf[s_]], dma_key="rf%d" % s_)
            for i in range(PF):
                load(i)
            for i, (cg, t) in enumerate(steps):
                if t == 0:
                    sdw, bdw = DP.get(u * 4 + cg)
                    wdv = sdw.rearrange("p (k n) -> p k n", k=UC)
                rows = slice(t * 128, (t + 1) * 128)
                s = (dcnt[0] + i) % NSTG
                bk = (dcnt[0] + i) % 8
                if i + PF < len(steps):
                    load(i + PF)

                def mm(e, bk=bk, wdv=wdv, rows=rows):
                    ins = None
                    for ci in range(UC):
                        ins = e.matmul(bank(bk), lhsT=actT[:, ci, rows], rhs=wdv[:, ci, :], start=(ci == 0), stop=(ci == UC - 1))
                    return ins
                pe(mm, [bdw, bactT], [PB[bk]])
                dve(lambda e, bk=bk, s=s: e.tensor_tensor(out=rtf_[s], in0=bank(bk), in1=rtf_[s], op=ALU.add), [PB[bk], brtf[s]], [brtf[s]])
                P.op("sp", lambda e, s=s, rows=rows, cg=cg, dst_d=dst_d: e.dma_start(out=dst_d[rows, cg * 512:(cg + 1) * 512], in_=rtf_[s]), reads=[brtf[s]], writes=[brun[t][cg]],
                     dma_key="ow%d" % s, final=(u == NU - 1))
                if t == NT - 1:
                    DP.release(u * 4 + cg)
            dcnt[0] += len(steps)
        P.emit()
    return nc


def _blkA(W, cols):
    sub = W[:, cols]
    n = sub.shape[1]
    return np.ascontiguousarray(sub.reshape(16, 128, n).transpose(1, 0, 2).reshape(128, 16 * n))


def _prep_weights(w_in, w_mem_kv, w_o, w_gate, w_up, w_down):
    r = np.arange
    blocks = []
    for j in range(4):
        blocks.append(_blkA(w_in, np.concatenate([r(j * 128, (j + 1) * 128), 512 + r(j * 128, (j + 1) * 128)])))
    for jj in range(2):
        blocks.append(_blkA(w_in, 1024 + r(jj * 256, (jj + 1) * 256)))
    for hp in range(4):
        for base in (1536, 2560, 3584):
            blocks.append(_blkA(w_in, base + r(hp * 256, (hp + 1) * 256)))
    for jj in range(2):
        blocks.append(_blkA(w_in, 4608 + r(jj * 256, (jj + 1) * 256)))
    for jj in range(4):
        blocks.append(_blkA(w_mem_kv, r(jj * 256, (jj + 1) * 256)))
    for half in range(2):
        for cg in range(4):
            sub = w_o[half * 1024:(half + 1) * 1024, cg * 512:(cg + 1) * 512]
            blocks.append(np.ascontiguousarray(sub.reshape(8, 128, 512).transpose(1, 0, 2).reshape(128, 4096)))
    wmix = np.stack(blocks, 0).astype(np.float32)
    assert wmix.shape == (32, 128, 4096)
    wgu = np.empty((88, 128, 2048), np.float32)
    for hc in range(NHC):
        wgu[2 * hc] = _blkA(w_gate, r(hc * 128, (hc + 1) * 128))
        wgu[2 * hc + 1] = _blkA(w_up, r(hc * 128, (hc + 1) * 128))
    wd = np.empty((16, 128, UC * 512), np.float32)
    for u in range(NU):
        for cg in range(4):
            sub = w_down[u * UC * 128:(u + 1) * UC * 128, cg * 512:(cg + 1) * 512]
            wd[u * 4 + cg] = sub.reshape(UC, 128, 512).transpose(1, 0, 2).reshape(128, UC * 512)
    return wmix, wgu, wd


def _prep_consts(g_mix, g_mem, conv_w, g_conv_out, g_dq, g_dk, lam_q1, lam_k1, lam_q2, lam_k2, g_sub, g_mq, g_mk, g_mem_out, g_ffn):
    c = np.zeros((128, NCST), np.float32)
    c[:, 0:128] = np.eye(128, dtype=np.float32)
    kk = np.arange(128)
    c[:, 128:256] = (kk[:, None] <= kk[None, :]).astype(np.float32)
    c[:, 256:272] = g_mix[0].reshape(16, 128).T
    c[:, 272:288] = g_mem[0].reshape(16, 128).T
    c[:, 288:304] = g_ffn[0].reshape(16, 128).T
    c[:, 304:316] = conv_w[0].reshape(3, 4, 128).transpose(2, 1, 0).reshape(128, 12)
    c[:, 316:320] = g_conv_out[0].reshape(4, 128).T
    c[:, 320] = g_mq[0]
    c[:, 321] = g_mk[0]
    c[:, 322:834] = np.concatenate([np.tile(g_dq[0], 4), np.tile(g_dk[0], 4)])[None, :]
    c[:, 834:962] = g_sub[0][None, :]
    c[:, 962:1474] = g_mem_out[0][None, :]
    c[:, 1474:1730] = np.concatenate([lam_q1[0], lam_k1[0], lam_q2[0], lam_k2[0]])[None, :]
    c[:, 1730] = g_sub[0]
    return c


_NC_CACHE = {}


def kernel(x, mem, positions, g_mix, g_mem, w_in, conv_w, g_conv_out, g_dq, g_dk,
           lam_q1, lam_k1, lam_q2, lam_k2, g_sub, w_mem_kv, g_mq, g_mk, g_mem_out,
           w_o, g_ffn, w_gate, w_up, w_down):
    f = lambda a: np.asarray(a, dtype=np.float32)
    x = f(x)
    mem = f(mem)
    positions = np.asarray(positions, dtype=np.int32)
    wmix, wgu, wd = _prep_weights(f(w_in)[0], f(w_mem_kv)[0], f(w_o)[0], f(w_gate)[0], f(w_up)[0], f(w_down)[0])
    cst = _prep_consts(f(g_mix), f(g_mem), f(conv_w), f(g_conv_out), f(g_dq), f(g_dk), f(lam_q1), f(lam_k1), f(lam_q2), f(lam_k2),
                       f(g_sub), f(g_mq), f(g_mk), f(g_mem_out), f(g_ffn))
    if "nc" not in _NC_CACHE:
        _NC_CACHE["nc"] = build()
    nc = _NC_CACHE["nc"]
    in_maps = []
    for b in range(8):
        in_maps.append({
            "x": np.ascontiguousarray(x[b]),
            "mem": np.ascontiguousarray(mem[b]),
            "pos": np.ascontiguousarray(positions[b].reshape(16, 128).T),
            "cst": cst, "wmix": wmix, "wgu": wgu, "wd": wd,
        })
    res = run_bass_kernel_spmd(nc, in_maps, core_ids=list(range(8)))
    return np.stack([r["out"] for r in res.results], 0).astype(np.float32)
```
